# Optimizing a Trainium2 kernel written in Bass

```python
import math
import jax, jax.numpy as jnp
from jax import lax
import numpy as np

D_MODEL = 1024
BATCH = 8
SEQ = 4096
DEPTH = 1

HEAD_DIM = 64
SWA_HEADS = 8
SWA_KV_HEADS = 2
SWA_WINDOW = 128
DSA_HEADS = 8
DSA_KV_RANK = 128
IDX_HEADS = 8
IDX_DIM = 64
IDX_TOPK_MAX = 256
DSA_QBLOCK = 128
REL_BUCKETS = 32
REL_MAX_DIST = 128
N_SELF_HEADS = SWA_HEADS + DSA_HEADS
MEM_LEN = 256
MEM_HEADS = 4
MEM_HEAD_DIM = D_MODEL // MEM_HEADS
D_FF = 4 * D_MODEL
DN_ALPHA = (2.0 * DEPTH) ** 0.25
DN_BETA = (8.0 * DEPTH) ** -0.25
LN_EPS = 1e-5
NEG_INF = -1e30
IN_SPLITS = (SWA_HEADS * HEAD_DIM,
             SWA_KV_HEADS * HEAD_DIM,
             SWA_KV_HEADS * HEAD_DIM,
             DSA_HEADS * HEAD_DIM,
             DSA_KV_RANK,
             IDX_HEADS * IDX_DIM,
             IDX_DIM,
             IDX_HEADS)
D_IN = sum(IN_SPLITS)
MIX_WIDTH = (SWA_HEADS + DSA_HEADS) * HEAD_DIM

kernel_name = "hybrid_swa_sink_dsa_deepnorm_block"


def layer_norm(x, g, b):
    xf = x.astype(jnp.float32)
    mu = jnp.mean(xf, axis=-1, keepdims=True)
    var = jnp.mean(jnp.square(xf - mu), axis=-1, keepdims=True)
    return ((xf - mu) * lax.rsqrt(var + LN_EPS) * g.astype(jnp.float32) + b.astype(jnp.float32)).astype(x.dtype)


def rms_norm(x, g):
    xf = x.astype(jnp.float32)
    ms = jnp.mean(jnp.square(xf), axis=-1, keepdims=True)
    return (xf * lax.rsqrt(ms + LN_EPS) * g.astype(jnp.float32)).astype(x.dtype)


def rel_bucket(dist):
    n = jnp.maximum(dist, 0)
    max_exact = REL_BUCKETS // 2
    nf = jnp.maximum(n, 1).astype(jnp.float32)
    large = max_exact + (jnp.log(nf / max_exact) / math.log(REL_MAX_DIST / max_exact)
                         * (REL_BUCKETS - max_exact)).astype(jnp.int32)
    large = jnp.minimum(large, REL_BUCKETS - 1)
    return jnp.where(n < max_exact, n, large)


def swa_sink_attention(q, k, v, sinks, rel_table):
    B, T, Hq, dh = q.shape
    W = SWA_WINDOW
    nb = T // W
    Hkv = SWA_KV_HEADS
    G = Hq // Hkv
    qb = q.reshape(B, nb, W, Hkv, G, dh)

    def band_blocks(z):
        zp = jnp.pad(z, ((0, 0), (W, 0), (0, 0), (0, 0)))
        prev = zp[:, :T].reshape(B, nb, W, Hkv, dh)
        cur = z.reshape(B, nb, W, Hkv, dh)
        return jnp.concatenate([prev, cur], axis=2)

    kb = band_blocks(k)
    vb = band_blocks(v)
    s = jnp.einsum('bnqkgd,bnskd->bnkgqs', qb, kb,
                   preferred_element_type=jnp.float32) * (dh ** -0.5)
    qi = jnp.arange(W)[:, None]
    si = jnp.arange(2 * W)[None, :]
    rel = qi + W - si
    band = (rel >= 0) & (rel < W)
    has_prev = (jnp.arange(nb) > 0)[:, None, None] | (si >= W)[None]
    valid = band[None] & has_prev
    bias = rel_table[rel_bucket(rel)].astype(jnp.float32)
    bias = jnp.transpose(bias, (2, 0, 1)).reshape(Hkv, G, W, 2 * W)
    logits = jnp.where(valid[None, :, None, None, :, :], s + bias, NEG_INF)
    sink = sinks.astype(jnp.float32).reshape(Hkv, G)[:, :, None, None]
    m = jnp.maximum(jnp.max(logits, axis=-1, keepdims=True), sink)
    p = jnp.exp(logits - m)
    denom = jnp.sum(p, axis=-1, keepdims=True) + jnp.exp(sink - m)
    o = jnp.einsum('bnkgqs,bnskd->bnqkgd', (p / denom).astype(v.dtype), vb)
    return o.reshape(B, T, Hq * dh)


def dsa_attention(q, c_kv, iq, ik, iw, w_uk, w_uv, rel_table):
    B, T, H, dh = q.shape
    topk = min(IDX_TOPK_MAX, T // 4)
    QB = DSA_QBLOCK
    nb = T // QB
    q_lat = jnp.einsum('bthd,hcd->bthc', q, w_uk) * (dh ** -0.5)
    key_pos = jnp.arange(T)

    def to_blocks(z):
        return jnp.moveaxis(z.reshape(B, nb, QB, *z.shape[2:]), 1, 0)

    def block(args):
        ql, iqb, iwb, n = args
        t = n * QB + jnp.arange(QB)
        dots = jnp.einsum('bqhd,bsd->bqhs', iqb, ik,
                          preferred_element_type=jnp.float32) * (IDX_DIM ** -0.5)
        score = jnp.einsum('bqh,bqhs->bqs', iwb.astype(jnp.float32), jax.nn.relu(dots))
        causal = key_pos[None, :] <= t[:, None]
        score = jnp.where(causal[None], score, NEG_INF)
        _, idx = lax.top_k(score, topk)
        ok = idx <= t[None, :, None]
        sel = jax.vmap(lambda c, i: c[i])(c_kv, idx)
        logits = jnp.einsum('bqhc,bqkc->bhqk', ql, sel, preferred_element_type=jnp.float32)
        bias = rel_table[rel_bucket(t[None, :, None] - idx)].astype(jnp.float32)
        logits = jnp.where(ok[:, None], logits + jnp.moveaxis(bias, 3, 1), NEG_INF)
        p = jax.nn.softmax(logits, axis=-1).astype(sel.dtype)
        o_lat = jnp.einsum('bhqk,bqkc->bqhc', p, sel)
        return jnp.einsum('bqhc,hcd->bqhd', o_lat, w_uv)

    out = lax.map(block, (to_blocks(q_lat), to_blocks(iq), to_blocks(iw), jnp.arange(nb)))
    return jnp.moveaxis(out, 0, 1).reshape(B, T, H * dh)


def memory_cross_attention(x, mem, wq, bq, wkv, bkv, wo, bo):
    B, T, _ = x.shape
    M = mem.shape[1]
    q = (x @ wq + bq).reshape(B, T, MEM_HEADS, MEM_HEAD_DIM)
    kv = mem @ wkv + bkv
    k, v = jnp.split(kv, 2, axis=-1)
    k = k.reshape(B, M, MEM_HEADS, MEM_HEAD_DIM)
    v = v.reshape(B, M, MEM_HEADS, MEM_HEAD_DIM)
    s = jnp.einsum('bthd,bmhd->bhtm', q, k, preferred_element_type=jnp.float32) * (MEM_HEAD_DIM ** -0.5)
    p = jax.nn.softmax(s, axis=-1).astype(v.dtype)
    o = jnp.einsum('bhtm,bmhd->bthd', p, v).reshape(B, T, D_MODEL)
    return o @ wo + bo


def setup_inputs(seed: int = 0) -> dict:
    key = jax.random.key(seed)
    ks = jax.random.split(key, 40)
    f32 = jnp.float32

    def w(k, shape, fan_in, scale=1.0):
        return jax.random.normal(k, shape, f32) * (fan_in ** -0.5) * scale

    def gain(k, shape):
        return 1.0 + 0.05 * jax.random.normal(k, shape, f32)

    def small(k, shape):
        return 0.01 * jax.random.normal(k, shape, f32)

    L = DEPTH
    return {
        "x": jax.random.normal(ks[0], (BATCH, SEQ, D_MODEL), f32),
        "mem": jax.random.normal(ks[1], (BATCH, MEM_LEN, D_MODEL), f32),
        "ln_emb_g": gain(ks[2], (D_MODEL,)),
        "ln_emb_b": small(ks[3], (D_MODEL,)),
        "w_in": w(ks[4], (L, D_MODEL, D_IN), D_MODEL),
        "b_in": small(ks[5], (L, D_IN)),
        "swa_sinks": 0.5 * jax.random.normal(ks[6], (L, SWA_HEADS), f32),
        "dsa_kv_norm_g": gain(ks[7], (L, DSA_KV_RANK)),
        "dsa_w_uk": w(ks[8], (L, DSA_HEADS, DSA_KV_RANK, HEAD_DIM), DSA_KV_RANK),
        "dsa_w_uv": w(ks[9], (L, DSA_HEADS, DSA_KV_RANK, HEAD_DIM), DSA_KV_RANK),
        "idx_k_ln_g": gain(ks[10], (L, IDX_DIM)),
        "idx_k_ln_b": small(ks[11], (L, IDX_DIM)),
        "rel_bias": 0.5 * jax.random.normal(ks[12], (REL_BUCKETS, N_SELF_HEADS), f32),
        "w_o": w(ks[13], (L, MIX_WIDTH, D_MODEL), MIX_WIDTH, DN_BETA),
        "b_o": small(ks[14], (L, D_MODEL)),
        "ln1_g": gain(ks[15], (L, D_MODEL)),
        "ln1_b": small(ks[16], (L, D_MODEL)),
        "xa_wq": w(ks[17], (L, D_MODEL, D_MODEL), D_MODEL),
        "xa_bq": small(ks[18], (L, D_MODEL)),
        "xa_wkv": w(ks[19], (L, D_MODEL, 2 * D_MODEL), D_MODEL),
        "xa_bkv": small(ks[20], (L, 2 * D_MODEL)),
        "xa_wo": w(ks[21], (L, D_MODEL, D_MODEL), D_MODEL, DN_BETA),
        "xa_bo": small(ks[22], (L, D_MODEL)),
        "ln2_g": gain(ks[23], (L, D_MODEL)),
        "ln2_b": small(ks[24], (L, D_MODEL)),
        "w_up": w(ks[25], (L, D_MODEL, D_FF), D_MODEL),
        "b_up": small(ks[26], (L, D_FF)),
        "w_down": w(ks[27], (L, D_FF, D_MODEL), D_FF, DN_BETA),
        "b_down": small(ks[28], (L, D_MODEL)),
        "ln3_g": gain(ks[29], (L, D_MODEL)),
        "ln3_b": small(ks[30], (L, D_MODEL)),
    }


def reference(x, mem, ln_emb_g, ln_emb_b, w_in, b_in, swa_sinks, dsa_kv_norm_g, dsa_w_uk, dsa_w_uv,
              idx_k_ln_g, idx_k_ln_b, rel_bias, w_o, b_o, ln1_g, ln1_b, xa_wq, xa_bq, xa_wkv, xa_bkv,
              xa_wo, xa_bo, ln2_g, ln2_b, w_up, b_up, w_down, b_down, ln3_g, ln3_b):
    B, T, _ = x.shape
    offsets = [int(o) for o in np.cumsum(IN_SPLITS)[:-1]]
    rel_a = rel_bias[:, :SWA_HEADS]
    rel_b = rel_bias[:, SWA_HEADS:]
    x = layer_norm(x, ln_emb_g, ln_emb_b)
    for l in range(DEPTH):
        h = x @ w_in[l] + b_in[l]
        qa, ka, va, qb, cb, iq, ik, iw = jnp.split(h, offsets, axis=-1)
        qa = qa.reshape(B, T, SWA_HEADS, HEAD_DIM)
        ka = ka.reshape(B, T, SWA_KV_HEADS, HEAD_DIM)
        va = va.reshape(B, T, SWA_KV_HEADS, HEAD_DIM)
        out_a = swa_sink_attention(qa, ka, va, swa_sinks[l], rel_a)
        qb = qb.reshape(B, T, DSA_HEADS, HEAD_DIM)
        cb = rms_norm(cb, dsa_kv_norm_g[l])
        iq = iq.reshape(B, T, IDX_HEADS, IDX_DIM)
        ik = layer_norm(ik, idx_k_ln_g[l], idx_k_ln_b[l])
        iw = iw * (IDX_HEADS ** -0.5)
        out_b = dsa_attention(qb, cb, iq, ik, iw, dsa_w_uk[l], dsa_w_uv[l], rel_b)
        mix = jnp.concatenate([out_a, out_b], axis=-1) @ w_o[l] + b_o[l]
        x = layer_norm(DN_ALPHA * x + mix, ln1_g[l], ln1_b[l])
        ca = memory_cross_attention(x, mem, xa_wq[l], xa_bq[l], xa_wkv[l], xa_bkv[l], xa_wo[l], xa_bo[l])
        x = layer_norm(DN_ALPHA * x + ca, ln2_g[l], ln2_b[l])
        f = jnp.square(jax.nn.relu(x @ w_up[l] + b_up[l])) @ w_down[l] + b_down[l]
        x = layer_norm(DN_ALPHA * x + f, ln3_g[l], ln3_b[l])
    return x
```

```python
import math
from contextlib import ExitStack

import numpy as np
import concourse.bass as bass
import concourse.mybir as mybir
from concourse.bass_utils import run_bass_kernel_spmd

F32 = mybir.dt.float32
BF16 = mybir.dt.bfloat16
AF = mybir.ActivationFunctionType
ALU = mybir.AluOpType
AX = mybir.AxisListType

T = 4096
D = 1024
NT = 32
KC = 8
DFF = 4096
DIN = 1992
ALPHA = 2.0 ** 0.25
LN_EPS = 1e-5
NEGM = -30000.0
NIT = 14
TOPK = 256
N_CORES = 8

ENGS = ("pe", "act", "dve", "pool", "sp")
EPOCH = 24000
NDMA_SEMS = 12


class Op:
    __slots__ = ("eng", "fn", "deps", "dma", "idx", "sig", "know", "has_dependents")

    def __init__(self, eng, fn, dma):
        self.eng = eng
        self.fn = fn
        self.dma = dma
        self.deps = {}
        self.sig = None
        self.know = None
        self.has_dependents = False


class Prog:
    def __init__(self, nc, ctx):
        self.nc = nc
        self.ctx = ctx
        self.ops = []
        self.res = {}
        self.engobj = {"pe": nc.tensor, "act": nc.scalar, "dve": nc.vector, "pool": nc.gpsimd, "sp": nc.sync}
        self.sems = {}
        self.ticks = {e: 0 for e in ENGS}
        self.dma_count = {e: 0 for e in ENGS}
        self.dma_last = {}
        self.know = {e: {} for e in ENGS}
        self.emitted = 0
        self.last_compute = {}
        self.dmas_since_barrier = []
        self.barrier_idx = None
        self.barrier_seen = set()

    def op(self, eng, fn, reads=(), writes=(), dma=False):
        o = Op(eng, fn, dma)
        o.idx = len(self.ops)
        deps = {}

        def add(d, raw):
            if d is None:
                return
            do = self.ops[d]
            if (not dma) and (not do.dma) and do.eng == eng:
                if not raw or eng == "pe":
                    return
            deps[d] = True

        for k in reads:
            r = self.res.setdefault(k, [None, []])
            add(r[0], True)
        for k in writes:
            r = self.res.setdefault(k, [None, []])
            add(r[0], False)
            for rd in r[1]:
                add(rd, False)
        for k in reads:
            self.res[k][1].append(o.idx)
        for k in writes:
            r = self.res[k]
            r[0] = o.idx
            r[1] = []
        if self.barrier_idx is not None and eng not in self.barrier_seen:
            deps[self.barrier_idx] = True
            self.barrier_seen.add(eng)
        o.deps = deps
        for d in deps:
            self.ops[d].has_dependents = True
        self.ops.append(o)
        if dma:
            self.dmas_since_barrier.append(o.idx)
        elif fn is not None:
            self.last_compute[eng] = o.idx
        return o

    def barrier(self):
        o = Op("sp", lambda e: e.nop(), False)
        o.idx = len(self.ops)
        deps = {}
        for e, i in self.last_compute.items():
            deps[i] = True
        for i in self.dmas_since_barrier:
            deps[i] = True
        if self.barrier_idx is not None:
            deps[self.barrier_idx] = True
        o.deps = deps
        for d in deps:
            self.ops[d].has_dependents = True
        o.has_dependents = True
        self.ops.append(o)
        self.barrier_idx = o.idx
        self.barrier_seen = {"sp"}
        self.dmas_since_barrier = []
        self.last_compute = {"sp": o.idx}
        self.res = {}

    def _sem(self, name):
        if name not in self.sems:
            self.sems[name] = self.ctx.enter_context(self.nc.semaphore(name))
        return self.sems[name]

    def emit(self):
        def merge(dst, src):
            for k, v in src.items():
                if dst.get(k, 0) < v:
                    dst[k] = v

        for o in self.ops[self.emitted:]:
            eng = o.eng
            eo = self.engobj[eng]
            deps = dict(o.deps)
            if o.dma:
                slot = self.dma_count[eng] % NDMA_SEMS
                prev = self.dma_last.get((eng, slot))
                if prev is not None:
                    deps[prev] = True
            kn = self.know[eng]
            need = {}
            for d in deps:
                src, val = self.ops[d].sig
                if kn.get(src, 0) < val and need.get(src, 0) < val:
                    need[src] = val
            for src, val in need.items():
                eo.wait_ge(self._sem(src), val)
            for d in deps:
                do = self.ops[d]
                src, val = do.sig
                if kn.get(src, 0) < val:
                    kn[src] = val
                merge(kn, do.know)
            inst = o.fn(eo) if o.fn is not None else None
            if o.dma:
                slot = self.dma_count[eng] % NDMA_SEMS
                n = self.dma_count[eng] // NDMA_SEMS + 1
                name = f"d_{eng}_{slot}"
                inst.then_inc(self._sem(name), 16)
                o.sig = (name, 16 * n)
                self.dma_last[(eng, slot)] = o.idx
                self.dma_count[eng] += 1
            elif o.has_dependents and inst is not None:
                self.ticks[eng] += 1
                ep = self.ticks[eng] // EPOCH
                val = self.ticks[eng] - ep * EPOCH
                if val == 0:
                    self.ticks[eng] += 1
                    val = 1
                name = f"c_{eng}_{ep}"
                inst.then_inc(self._sem(name), 1)
                o.sig = (name, val)
            else:
                o.sig = ("none", 0)
            o.know = dict(kn)
            o.fn = None
        self.emitted = len(self.ops)


def _rel_bucket_np(n):
    n = np.maximum(n, 0)
    me = 16
    nf = np.maximum(n, 1).astype(np.float32)
    large = me + (np.log(nf / np.float32(me)) / np.float32(math.log(128 / 16)) * np.float32(16)).astype(np.int32)
    large = np.minimum(large, 31)
    return np.where(n < me, n, large)


def _constants():
    c = {}
    c["c_ident"] = np.eye(128, dtype=np.float32)
    c["c_exch"] = np.ascontiguousarray(np.eye(128, dtype=np.float32)[::-1])
    ohv = np.zeros((32, 2, 384), np.float32)
    rel = np.arange(384) - 127
    bk = _rel_bucket_np(rel)
    for i in range(384):
        if rel[i] >= 0:
            ohv[bk[i], 0, i] = 1.0
            ohv[bk[i], 1, i] += 1.0
            ohv[31, 1, i] -= 1.0
    c["c_ohv"] = ohv
    p = np.arange(128)[:, None]
    q = np.arange(128)[None, :]
    m = np.zeros((128, 2, 2, 128), np.float32)
    cur = np.where(q >= p, 0.0, NEGM)
    m[:, 0, 0, :] = cur
    m[:, 1, 0, :] = cur
    m[:, 0, 1, :] = np.where(q < p, 0.0, NEGM)
    m[:, 1, 1, :] = 0.0
    c["c_maskT"] = m
    qq = np.arange(128)[:, None]
    ss = np.arange(128)[None, :]
    c["c_cmaskq"] = np.where(ss <= qq, 0.0, -1e30).astype(np.float32)
    c["c_pow2"] = np.tile((0.5 ** (np.arange(NIT) + 1)).astype(np.float32)[None, :], (128, 1))
    return c


CONST_SHAPES = {
    "c_ident": [128, 128], "c_exch": [128, 128], "c_ohv": [32, 2, 384], "c_maskT": [128, 2, 2, 128],
    "c_cmaskq": [128, 128], "c_pow2": [128, NIT],
}

PARAM_SHAPES = {
    "x": [T, D], "mem": [256, D], "ln_emb_g": [D], "ln_emb_b": [D], "w_in": [D, DIN], "b_in": [DIN],
    "swa_sinks": [8], "dsa_kv_norm_g": [128], "dsa_w_uk": [8, 128, 64], "dsa_w_uv": [8, 128, 64],
    "idx_k_ln_g": [64], "idx_k_ln_b": [64], "rel_bias": [32, 16], "w_o": [D, D], "b_o": [D],
    "ln1_g": [D], "ln1_b": [D], "xa_wq": [D, D], "xa_bq": [D], "xa_wkv": [D, 2 * D], "xa_bkv": [2 * D],
    "xa_wo": [D, D], "xa_bo": [D], "ln2_g": [D], "ln2_b": [D], "w_up": [D, DFF], "b_up": [DFF],
    "w_down": [DFF, D], "b_down": [D], "ln3_g": [D], "ln3_b": [D],
}


class _Done(Exception):
    pass


def build_nc(n_tiles=NT, dbg=99):
    nc = bass.Bass("TRN2", target_bir_lowering=False)
    dr = {}
    for k, shp in PARAM_SHAPES.items():
        dr[k] = nc.dram_tensor(k, shp, F32, kind="ExternalInput")
    for k, shp in CONST_SHAPES.items():
        dr[k] = nc.dram_tensor(k, shp, F32, kind="ExternalInput")
    y_d = nc.dram_tensor("y", [T, D], F32, kind="ExternalOutput")
    x1_d = nc.dram_tensor("x1_scr", [T, D], F32)
    x2_d = nc.dram_tensor("x2_scr", [T, D], F32)
    vec_d = nc.dram_tensor("vec_scr", [16, 384], F32)

    def bcast_rows(name, n, off=0, parts=128):
        return bass.AP(dr[name], off, [[0, parts], [1, n]])

    ctx = ExitStack()
    with ctx:
        P = Prog(nc, ctx)

        def sbt(stack, name, shape, dt):
            return stack.enter_context(nc.sbuf_tensor(name, shape, dt))

        PB = [ctx.enter_context(nc.psum_tensor(f"PB{i}", [128, 1024], F32)) for i in range(4)]

        def banks(i, lo=0, hi=1024):
            ks = []
            if lo < 512:
                ks.append(f"B{2 * i}")
            if hi > 512:
                ks.append(f"B{2 * i + 1}")
            return ks

        def MM(out, lhsT, rhs, start, stop, reads, writes):
            P.op("pe", lambda e: e.matmul(out, lhsT=lhsT, rhs=rhs, start=start, stop=stop), reads, writes)

        def ACT(out, in_, func, reads, writes, bias=None, scale=1.0, accum_out=None):
            kw = {}
            if bias is not None:
                kw["bias"] = bias
            if accum_out is not None:
                kw["accum_out"] = accum_out
            P.op("act", lambda e: e.activation(out=out, in_=in_, func=func, scale=scale, **kw), reads, writes)

        def TS(eng, out, in0, s1, s2, op0, op1, reads, writes, accum_out=None):
            if accum_out is not None:
                P.op(eng, lambda e: e.tensor_scalar(out=out, in0=in0, scalar1=s1, scalar2=s2, op0=op0, op1=op1,
                                                    accum_out=accum_out), reads, writes)
            elif op1 is None:
                P.op(eng, lambda e: e.tensor_scalar(out=out, in0=in0, scalar1=s1, scalar2=None, op0=op0), reads, writes)
            else:
                P.op(eng, lambda e: e.tensor_scalar(out=out, in0=in0, scalar1=s1, scalar2=s2, op0=op0, op1=op1),
                     reads, writes)

        def TT(eng, out, in0, in1, op, reads, writes):
            P.op(eng, lambda e: e.tensor_tensor(out=out, in0=in0, in1=in1, op=op), reads, writes)

        def STT(out, in0, scalar, in1, op0, op1, reads, writes):
            P.op("dve", lambda e: e.scalar_tensor_tensor(out=out, in0=in0, scalar=scalar, in1=in1, op0=op0, op1=op1),
                 reads, writes)

        def CP(eng, out, in_, reads, writes):
            P.op(eng, lambda e: e.tensor_copy(out=out, in_=in_), reads, writes)

        def DMA(q, out, in_, reads, writes, **kw):
            P.op(q, lambda e: e.dma_start(out=out, in_=in_, **kw), reads, writes, dma=True)

        def BNS(out, in_, reads, writes):
            P.op("dve", lambda e: e.bn_stats(out=out, in_=in_), reads, writes)

        def BNA(out, in_, reads, writes):
            P.op("dve", lambda e: e.bn_aggr(out=out, in_=in_), reads, writes)

        def RED(out, in_, op, reads, writes):
            P.op("dve", lambda e: e.tensor_reduce(out=out, in_=in_, axis=AX.X, op=op), reads, writes)

        def RECIP(out, in_, reads, writes):
            P.op("dve", lambda e: e.reciprocal(out=out, in_=in_), reads, writes)

        def RECIP_ACT(out, in_, tmp, reads, writes):
            ACT(tmp, in_, AF.Ln, reads, writes)
            ACT(out, tmp, AF.Exp, [], writes, scale=-1.0)

        def MEMSET(eng, ap, val, writes):
            P.op(eng, lambda e: e.memset(ap, val), (), writes)

        def run_pipeline(make, count, width, stagger):
            active = []
            nxt = 0
            since = stagger
            while nxt < count or active:
                if nxt < count and len(active) < width and (since >= stagger or not active):
                    active.append(make(nxt))
                    nxt += 1
                    since = 0
                alive = []
                for g in active:
                    try:
                        next(g)
                        alive.append(g)
                    except StopIteration:
                        pass
                active = alive
                since += 1

        ident_f = sbt(ctx, "ident_f", [128, 128], F32)
        ident_b = sbt(ctx, "ident_b", [128, 128], BF16)
        ident4 = sbt(ctx, "ident4", [128, 4, 128], BF16)
        ones_b = sbt(ctx, "ones_b", [128, 128], BF16)
        ones_f = sbt(ctx, "ones_f", [64, 128], F32)
        eps_t = sbt(ctx, "eps_t", [128, 1], F32)
        DMA("sp", ident_f[:], dr["c_ident"].ap(), (), ["ident_f"])
        CP("dve", ident_b[:], ident_f[:], ["ident_f"], ["ident_b"])
        for r in range(4):
            CP("dve", ident4[:, r, :], ident_f[:], ["ident_f"], ["ident4"])
        MEMSET("dve", ones_b[:], 1.0, ["ones_b"])
        MEMSET("dve", ones_f[:], 1.0, ["ones_f"])
        MEMSET("dve", eps_t[:], LN_EPS, ["eps_t"])

        def rstd_from_var(var_ap, out_ap, tmp_ap, rkeys, wkeys, scale=1.0):
            ACT(tmp_ap, var_ap, AF.Ln, rkeys + ["eps_t"], wkeys, bias=eps_t[:], scale=scale)
            ACT(out_ap, tmp_ap, AF.Exp, wkeys, wkeys, scale=-0.5)

        def layernorm(src, src_keys, dst, dst_key, g_bc, b_bc, gb_keys, st, st_key, affine=True, aff_eng="pool"):
            wk = [st_key]
            BNS(st[:, 0:6], src[:, 0:512], src_keys, wk)
            BNS(st[:, 6:12], src[:, 512:1024], src_keys + wk, wk)
            BNA(st[:, 12:14], st[:, 0:12], wk, wk)
            rstd_from_var(st[:, 13:14], st[:, 15:16], st[:, 14:15], wk, wk)
            TS("dve", dst, src, st[:, 12:13], st[:, 15:16], ALU.subtract, ALU.mult, src_keys + wk, [dst_key])
            if affine:
                TT(aff_eng, dst, dst, g_bc, ALU.mult, [dst_key] + gb_keys, [dst_key])
                TT(aff_eng, dst, dst, b_bc, ALU.add, [dst_key] + gb_keys, [dst_key])

        def to_feature_major(src, src_key, bf_tmp, bf_key, dstT, dstT_key, pb, evac, cast_eng="act"):
            if cast_eng is None:
                pass
            elif cast_eng == "act":
                ACT(bf_tmp, src, AF.Identity, [src_key], [bf_key])
            else:
                CP(cast_eng, bf_tmp, src, [src_key], [bf_key])
            for kc in range(KC):
                MM(PB[pb][:, kc * 128:(kc + 1) * 128], bf_tmp[:, kc * 128:(kc + 1) * 128], ident_b[:], True, True,
                   [bf_key, "ident_b"], banks(pb, kc * 128, (kc + 1) * 128))
            for hf in range(2):
                eng = evac[hf]
                o = dstT[:, 4 * hf:4 * hf + 4, :]
                i = PB[pb][:, 512 * hf:512 * hf + 512].rearrange("p (a b) -> p a b", a=4)
                if eng == "act":
                    ACT(o, i, AF.Identity, [], banks(pb, 512 * hf, 512 * hf + 512) + [dstT_key])
                else:
                    CP("dve", o, i, [], banks(pb, 512 * hf, 512 * hf + 512) + [dstT_key])

        def make_bias_hl(hl, tstack, name, pieces, n):
            f = sbt(tstack, name + "_f", [1, n], F32)
            hb = sbt(tstack, name + "_h", [1, n], BF16)
            lb = sbt(tstack, name + "_l", [1, n], BF16)
            for (d0, d1, src) in pieces:
                DMA("sp", f[:, d0:d1], src, (), [name + "_f"])
            CP("dve", hb[:], f[:], [name + "_f"], [name + "_h"])
            TT("dve", f[:], f[:], hb[:], ALU.subtract, [name + "_f", name + "_h"], [name + "_f"])
            CP("dve", lb[:], f[:], [name + "_f"], [name + "_l"])
            DMA("sp", hl[0:1, :], hb[:], [name + "_h"], [name])
            DMA("sp", hl[1:2, :], lb[:], [name + "_l"], [name])
            return hl

        def bias_mm(o, hl, lo, hi, key, bk):
            MM(o, ones_b[0:2, :], hl[0:2, lo:hi], False, True, ["ones_b", key], bk)

        def load_bcast(stack, name, src_name, n, off=0):
            t = sbt(stack, name, [128, n], F32)
            DMA("sp", t[:], bcast_rows(src_name, n, off), (), [name])
            return t

        def load_weight_bf16(stack, name, src_name, ncols, col_lo=0, col_hi=None, rows=D):
            if col_hi is None:
                col_hi = ncols
            nk = rows // 128
            w = col_hi - col_lo
            t = sbt(stack, name, [128, nk, w], BF16)
            src = dr[src_name].ap().rearrange("(k p) n -> p k n", p=128)
            for c0 in range(0, w, 1024):
                c1 = min(w, c0 + 1024)
                for k0 in range(0, nk, 4):
                    k1 = min(nk, k0 + 4)
                    DMA("pool", t[:, k0:k1, c0:c1], src[:, k0:k1, col_lo + c0:col_lo + c1], (),
                        [f"{name}:{k0 // 4}:{c0 // 1024}"])
            return t

        def load_weight_into(t, name, src_name, w, rows):
            nk = rows // 128
            src = dr[src_name].ap().rearrange("(k p) n -> p k n", p=128)
            for k0 in range(0, nk, 4):
                k1 = min(nk, k0 + 4)
                for c0 in range(0, w, 1024):
                    c1 = min(w, c0 + 1024)
                    DMA("pool", t[:, k0:k1, c0:c1], src[:, k0:k1, c0:c1], (), [f"{name}:{k0 // 4}:{c0 // 1024}"])

        def wkeys(name, kc, lo, hi):
            return [f"{name}:{kc // 4}:{c}" for c in range(lo // 1024, (hi - 1) // 1024 + 1)]

        with ExitStack() as sA:
            w_fm = sbt(sA, "w_fm", [128, KC, 1280], BF16)
            w_tm = sbt(sA, "w_tm", [128, KC, 840], BF16)
            win = dr["w_in"].ap().rearrange("(k p) n -> p k n", p=128)
            for k0 in range(0, KC, 4):
                ks = slice(k0, k0 + 4)
                DMA("pool", w_fm[:, ks, 0:640], win[:, ks, 0:640], (), ["w_fm"])
                DMA("pool", w_fm[:, ks, 640:704], win[:, ks, 576:640], (), ["w_fm"])
                DMA("pool", w_fm[:, ks, 704:768], win[:, ks, 512:576], (), ["w_fm"])
                DMA("pool", w_fm[:, ks, 768:1280], win[:, ks, 768:1280], (), ["w_fm"])
                DMA("pool", w_tm[:, ks, 0:512], win[:, ks, 1408:1920], (), ["w_tm"])
                DMA("pool", w_tm[:, ks, 512:640], win[:, ks, 640:768], (), ["w_tm"])
                DMA("pool", w_tm[:, ks, 640:768], win[:, ks, 1280:1408], (), ["w_tm"])
                DMA("pool", w_tm[:, ks, 768:840], win[:, ks, 1920:1992], (), ["w_tm"])
            w_o_sb = load_weight_bf16(sA, "w_o_sb", "w_o", D)
            wuv = sbt(sA, "wuv", [128, 8, 64], BF16)
            DMA("pool", wuv[:], dr["dsa_w_uv"].ap().rearrange("h c d -> c h d"), (), ["wuv"])

            bfm = sbt(sA, "bfm", [128, 10], F32)
            bin_ap = dr["b_in"].ap()
            fm_srcs = [(0, 512, 0), (512, 640, 4), (768, 1280, 6)]
            for lo, hi, j0 in fm_srcs:
                DMA("sp", bfm[:, j0:j0 + (hi - lo) // 128], bin_ap[lo:hi].rearrange("(j p) -> p j", p=128), (),
                    ["bfm"], allow_slow_non_contiguous=True)
            DMA("sp", bfm[0:64, 5:6], bin_ap[576:640].rearrange("(j p) -> p j", p=64), (), ["bfm"],
                allow_slow_non_contiguous=True)
            DMA("sp", bfm[64:128, 5:6], bin_ap[512:576].rearrange("(j p) -> p j", p=64), (), ["bfm"],
                allow_slow_non_contiguous=True)

            g0_bc = load_bcast(sA, "g0_bc", "ln_emb_g", D)
            b0_bc = load_bcast(sA, "b0_bc", "ln_emb_b", D)
            gkv_bc = load_bcast(sA, "gkv_bc", "dsa_kv_norm_g", 128)
            gik_bc = load_bcast(sA, "gik_bc", "idx_k_ln_g", 64)
            bik_bc = load_bcast(sA, "bik_bc", "idx_k_ln_b", 64)
            esink = load_bcast(sA, "esink", "swa_sinks", 8)
            ACT(esink[:], esink[:], AF.Exp, ["esink"], ["esink"])
            cmaskq = sbt(sA, "cmaskq", [128, 128], F32)
            DMA("sp", cmaskq[:], dr["c_cmaskq"].ap(), (), ["cmaskq"])
            pow2 = sbt(sA, "pow2", [128, NIT], F32)
            DMA("sp", pow2[:], dr["c_pow2"].ap(), (), ["pow2"])

            swaBT = sbt(sA, "swaBT", [128, 8, 2, 128], F32)
            dsaBT = sbt(sA, "dsaBT", [128, 8, 2, 128], F32)
            wukT = sbt(sA, "wukT", [128, 4, 128], BF16)
            ckv = sbt(sA, "ckv", [128, NT, 128], BF16)
            ckvT = sbt(sA, "ckvT", [128, T], BF16)
            ikT2 = sbt(sA, "ikT2", [128, T], BF16)
            kaTr = sbt(sA, "kaTr", [128, 2, 2, 128], BF16)
            va1r = sbt(sA, "va1r", [128, 2, 2, 65], BF16)
            MEMSET("dve", va1r[:], 1.0, ["va1r"])

            hl_tm = sbt(sA, "hl_tm", [2, 840], BF16)
            hl_o = sbt(sA, "hl_o", [2, D], BF16)
            with ExitStack() as sS:
                b2 = bin_ap.rearrange("(o n) -> o n", o=1)
                make_bias_hl(hl_tm, sS, "hl_tm", [(0, 512, b2[:, 1408:1920]), (512, 640, b2[:, 640:768]),
                                                     (640, 768, b2[:, 1280:1408]), (768, 840, b2[:, 1920:1992])], 840)
                make_bias_hl(hl_o, sS, "hl_o", [(0, D, dr["b_o"].ap().rearrange("(o n) -> o n", o=1))], D)
                tab = sbt(sS, "tab", [32, 16], F32)
                ohv = sbt(sS, "ohv", [32, 2, 384], F32)
                exch = sbt(sS, "exch", [128, 128], F32)
                maskT = sbt(sS, "maskT", [128, 2, 2, 128], F32)
                vecs = sbt(sS, "vecs", [16, 2, 384], F32)
                Hk = sbt(sS, "Hk", [128, 16, 2, 128], F32)
                wuk_f = sbt(sS, "wuk_f", [128, 8, 64], F32)
                DMA("sp", tab[:], dr["rel_bias"].ap(), (), ["tab"])
                DMA("sp", ohv[:], dr["c_ohv"].ap(), (), ["ohv"])
                DMA("sp", exch[:], dr["c_exch"].ap(), (), ["exch"])
                DMA("sp", maskT[:], dr["c_maskT"].ap(), (), ["maskT"])
                DMA("sp", wuk_f[:], dr["dsa_w_uk"].ap().rearrange("h c d -> c h d"), (), ["wuk_f"])
                for g in range(2):
                    MM(PB[g][0:16, 0:384], tab[:], ohv[:, g, :], True, True, ["tab", "ohv"], banks(g, 0, 384))
                    CP("dve", vecs[:, g, :], PB[g][0:16, 0:384], [], banks(g, 0, 384) + ["vecs"])
                DMA("sp", vec_d.ap()[0:8, :], vecs[0:8, 0, :], ["vecs"], ["vec_d"])
                DMA("sp", vec_d.ap()[8:16, :], vecs[8:16, 1, :], ["vecs"], ["vec_d"])
                for h0 in range(0, 16, 4):
                    src = bass.AP(vec_d, h0 * 384, [[1, 128], [384, 4], [128, 2], [1, 128]])
                    DMA("sp", Hk[:, h0:h0 + 4, :, :], src, ["vec_d"], ["Hk"])
                for blk in range(8):
                    pb = blk % 4
                    MM(PB[pb][:, 0:512], exch[:], Hk[:, 2 * blk:2 * blk + 2, :, :].rearrange("p h c q -> p (h c q)"),
                       True, True, ["exch", "Hk"], banks(pb, 0, 512))
                    g = 0 if blk < 4 else 1
                    for i in range(2):
                        if blk < 4:
                            dsl = swaBT[:, i * 4 + blk, :, :]
                        else:
                            dsl = dsaBT[:, 2 * (blk - 4) + i, :, :]
                        TT("dve", dsl, PB[pb][:, i * 256:(i + 1) * 256].rearrange("p (c q) -> p c q", c=2),
                           maskT[:, g, :, :], ALU.add, ["maskT"], banks(pb, 0, 512) + ["swaBT", "dsaBT"])
                for j in range(4):
                    MM(PB[j][:, 0:128], wuk_f[:, 2 * j:2 * j + 2, :].rearrange("p h d -> p (h d)"), ident_f[:],
                       True, True, ["wuk_f", "ident_f"], banks(j, 0, 128))
                    CP("dve", wukT[:, j, :], PB[j][:, 0:128], [], banks(j, 0, 128) + ["wukT"])
                P.barrier()
                P.emit()

            xin = [sbt(sA, "xin0", [128, D], F32)] * 2
            XN = [sbt(sA, f"XN{i}", [128, D], F32) for i in range(3)]
            Y = [sbt(sA, "Y0", [128, D], F32)] * 2
            bfA = sbt(sA, "bfA", [128, D], BF16)
            xT = sbt(sA, "xT", [128, KC, 128], BF16)
            AO = [sbt(sA, f"AO{i}", [128, D], BF16) for i in range(3)]
            aoT = sbt(sA, "aoT", [128, KC, 128], BF16)
            qaT = sbt(sA, "qaT", [128, 4, 128], BF16)
            qbT = sbt(sA, "qbT", [128, 4, 128], BF16)
            qlatT = [sbt(sA, f"qlatT{i}", [128, 8, 128], BF16) for i in range(3)]
            iqs = sbt(sA, "iqs", [128, 512], BF16)
            iqT = sbt(sA, "iqT", [128, 4, 128], BF16)
            ik2 = sbt(sA, "ik2", [128, 128], BF16)
            ikf = sbt(sA, "ikf", [128, 64], F32)
            sgnD = sbt(sA, "sgnD", [128, 8, 128], BF16)
            Rb = [sbt(sA, f"Rb{i}", [128, 512], BF16) for i in range(4)]
            score = sbt(sA, "score", [128, T], F32)
            NM = [sbt(sA, f"NM{i}", [128, T], BF16) for i in range(2)]
            LfS = sbt(sA, "LfS", [128, 1024], F32)
            PTS = sbt(sA, "PTS", [128, 2, 1024], BF16)
            PTD = sbt(sA, "PTD", [128, 3, 512], BF16)
            rs = sbt(sA, "rs", [128, 512], F32)
            olat = sbt(sA, "olat", [128, 1024], BF16)
            stF = sbt(sA, "stF", [128, 24], F32)
            stB = sbt(sA, "stB", [128, 24], F32)
            sm = sbt(sA, "sm", [128, 64], F32)
            bis = sbt(sA, "bis", [128, 8 + NIT], F32)
            print("pass A sbuf bytes remaining", nc.sbuf_bytes_remaining)

            x_ap = dr["x"].ap()
            PBt = PB[2]
            tmA = ["B5"]
            tmB = ["B6"]

            def front(n, part):
                slot = n % 2
                s2 = n % 3
                do_topk = n >= 2
                L = (n + 1) * 128
                if part == 2:
                    yield from front_indexer(n, do_topk, L)
                    return
                xi = xin[0]
                xk = "xin0"
                xn = XN[s2]
                xnk = f"XN{s2}"
                if part == 0:
                    DMA("sp", xi[:], x_ap[n * 128:(n + 1) * 128, :], (), [xk])
                    yield
                    layernorm(xi[:], [xk], xn[:], xnk, g0_bc[:], b0_bc[:], ["g0_bc", "b0_bc"], stF, "stF",
                              aff_eng="dve")
                    yield
                    ACT(bfA[:], xn[:], AF.Identity, [xnk], ["bfA"])
                    yield
                    return
                to_feature_major(xn[:], xnk, bfA[:], "bfA", xT, "xT", 2, ("act", "dve"), cast_eng=None)
                yield
                def fm_dst(j):
                    if j < 4:
                        return qaT[:, j, :], "qaT"
                    if j == 4:
                        return kaTr[:, 0, slot, :], f"kaTr{slot}"
                    if j == 5:
                        return kaTr[:, 1, slot, :], f"kaTr{slot}"
                    return qbT[:, j - 6, :], "qbT"

                def fm_chunk(j, pb, off):
                    for kc in range(KC):
                        MM(PB[pb][:, off:off + 128], w_fm[:, kc, j * 128:(j + 1) * 128], xT[:, kc, :], kc == 0,
                           kc == KC - 1, ["w_fm", "xT"], banks(pb, off, off + 128))

                def fm_evac(j, pb, off):
                    dst, dk = fm_dst(j)
                    if j % 2 == 0:
                        ACT(dst, PB[pb][:, off:off + 128], AF.Identity, ["bfm"], banks(pb, off, off + 128) + [dk],
                            bias=bfm[:, j:j + 1])
                    else:
                        TS("dve", dst, PB[pb][:, off:off + 128], bfm[:, j:j + 1], None, ALU.add, None, ["bfm"],
                           banks(pb, off, off + 128) + [dk])
                for j in range(8):
                    fm_chunk(j, 3, j * 128)
                    if j % 4 == 3:
                        yield
                for j in range(8, 10):
                    fm_chunk(j, 2, (j - 8) * 128)
                for j in range(10):
                    if j < 8:
                        fm_evac(j, 3, j * 128)
                    else:
                        fm_evac(j, 2, (j - 8) * 128)
                yield
                for (lo, hi, pb, off) in ((0, 512, 2, 512), (512, 840, 3, 0)):
                    w = hi - lo
                    o = PB[pb][:, off:off + w]
                    for kc in range(KC):
                        MM(o, xT[:, kc, :], w_tm[:, kc, lo:hi], kc == 0, False, ["xT", "w_tm"],
                           banks(pb, off, off + w))
                    bias_mm(o, hl_tm, lo, hi, "hl_tm", banks(pb, off, off + w))
                yield
                IQ = PB[2][:, 512:1024]
                TMB = PB[3]
                qb_ = {0: (3, 512), 1: (2, 0)}
                for h in range(8):
                    e = h % 2
                    pb, base = qb_[e]
                    off = base + (h // 2) * 128
                    MM(PB[pb][:, off:off + 128], wukT[64 * e:64 * e + 64, h // 2, :],
                       qbT[64 * e:64 * e + 64, h // 2, :], True, True, ["wukT", "qbT"], banks(pb, off, off + 128))
                for e in range(2):
                    pb, base = qb_[e]
                    ACT(qlatT[s2][:].rearrange("p (j e) q -> p e j q", e=2)[:, e, :, :],
                        PB[pb][:, base:base + 512].rearrange("p (j q) -> p j q", j=4), AF.Identity, [],
                        banks(pb, base, base + 512) + [f"qlatT{s2}"], scale=0.125)
                yield
                CP("dve", va1r[:, slot, :, 0:64], TMB[:, 0:128].rearrange("p (k d) -> p k d", k=2), [],
                   tmB + [f"va1r{slot}"])
                BNS(sm[:, 56:62], TMB[:, 128:256], ["sm"], tmB + ["sm"])
                BNA(sm[:, 62:64], sm[:, 56:62], ["sm"], ["sm"])
                STT(sm[:, 0:1], sm[:, 62:63], sm[:, 62:63], sm[:, 63:64], ALU.mult, ALU.add, ["sm"], ["sm"])
                rstd_from_var(sm[:, 0:1], sm[:, 2:3], sm[:, 1:2], ["sm"], ["sm"])
                STT(ckv[:, n, :], TMB[:, 128:256], sm[:, 2:3], gkv_bc[:], ALU.mult, ALU.mult, ["sm", "gkv_bc"],
                    tmB + [f"ckv{n}"])
                MM(PB[3][:, 512:640], ckv[:, n, :], ident_b[:], True, True, [f"ckv{n}", "ident_b"], ["B7"])
                CP("dve", ckvT[:, n * 128:(n + 1) * 128], PB[3][:, 512:640], [], ["B7", f"ckvT{n}"])
                BNS(sm[:, 8:14], TMB[:, 256:320], ["sm"], tmB + ["sm"])
                BNA(sm[:, 14:16], sm[:, 8:14], ["sm"], ["sm"])
                rstd_from_var(sm[:, 15:16], sm[:, 17:18], sm[:, 16:17], ["sm"], ["sm"])
                TS("dve", ikf[:], TMB[:, 256:320], sm[:, 14:15], sm[:, 17:18], ALU.subtract, ALU.mult, ["sm"],
                   tmB + ["ikf"])
                TT("dve", ikf[:], ikf[:], gik_bc[:], ALU.mult, ["ikf", "gik_bc"], ["ikf"])
                TT("dve", ik2[:, 0:64], ikf[:], bik_bc[:], ALU.add, ["ikf", "bik_bc"], ["ik2"])
                CP("dve", ik2[:, 64:128], ik2[:, 0:64], ["ik2"], ["ik2"])
                MM(PB[3][:, 640:768], ik2[:], ident_b[:], True, True, ["ik2", "ident_b"], ["B7"])
                CP("dve", ikT2[:, n * 128:(n + 1) * 128], PB[3][:, 640:768], [], ["B7", f"ikT{n}"])
                yield
                do_topk = n >= 2
                if do_topk:
                    ACT(sm[:, 24:32], TMB[:, 320:328], AF.Abs, ["sm"], tmB + ["sm"])
                    ACT(sm[:, 32:40], TMB[:, 320:328], AF.Sign, ["sm"], tmB + ["sm"])
                    TT("dve", iqs[:].rearrange("p (h d) -> p h d", h=8), IQ.rearrange("p (h d) -> p h d", h=8),
                       sm[:, 24:32].rearrange("p (h o) -> p h o", o=1).to_broadcast([128, 8, 64]), ALU.mult,
                       ["sm"], tmA + ["iqs"])
                    TT("dve", sgnD[:], ident_b[:].rearrange("p (o q) -> p o q", o=1).to_broadcast([128, 8, 128]),
                       sm[:, 32:40].rearrange("p (h o) -> p h o", o=1).to_broadcast([128, 8, 128]), ALU.mult,
                       ["sm", "ident_b"], ["sgnD"])
                    for j in range(4):
                        MM(PB[2][:, j * 128:(j + 1) * 128], iqs[:, j * 128:(j + 1) * 128], ident_b[:], True, True,
                           ["iqs", "ident_b"], ["B4"])
                    ACT(iqT[:].rearrange("p j q -> p (j q)"), PB[2][:, 0:512], AF.Identity, [], ["B4", "iqT"])
                    yield
                chunks = [(0, n)] + ([(1, n - 1)] if n >= 1 else [])
                for (c, kt) in chunks:
                    ks = kt % 2
                    for h in range(8):
                        e = h % 2
                        kv = h // 4
                        arr = 0 if kv == e else 1
                        hh = e * 4 + h // 2
                        MM(PB[3][:, hh * 128:(hh + 1) * 128], kaTr[64 * e:64 * e + 64, arr, ks, :],
                           qaT[64 * e:64 * e + 64, h // 2, :], True, True, [f"kaTr{ks}", "qaT"],
                           banks(3, hh * 128, (hh + 1) * 128))
                    STT(LfS[:].rearrange("p (h q) -> p h q", h=8), PB[3][:].rearrange("p (h q) -> p h q", h=8),
                        0.125, swaBT[:, :, c, :], ALU.mult, ALU.add, ["swaBT"], ["B6", "B7", "LfS"])
                    ACT(PTS[:, c, :], LfS[:], AF.Exp, ["LfS"], [f"PTS{c}"])
                    yield
                for h in range(8):
                    kv = h // 4
                    hh = (h % 2) * 4 + h // 2
                    for ci, (c, kt) in enumerate(chunks):
                        MM(PB[2][:, h * 128:h * 128 + 65], PTS[:, c, hh * 128:(hh + 1) * 128],
                           va1r[:, kt % 2, kv, :], ci == 0, ci == len(chunks) - 1, [f"PTS{c}", f"va1r{kt % 2}"],
                           banks(2, h * 128, h * 128 + 65))
                pv = PB[2][:].rearrange("p (h x) -> p h x", h=8)
                TT("dve", sm[:, 40:48].rearrange("p (h o) -> p h o", o=1), pv[:, :, 64:65],
                   esink[:].rearrange("p (h o) -> p h o", o=1), ALU.add, ["esink", "sm"], ["B4", "B5", "sm"])
                RECIP(sm[:, 48:56], sm[:, 40:48], ["sm"], ["sm"])
                TT("dve", AO[s2][:, 0:512].rearrange("p (h d) -> p h d", h=8), pv[:, :, 0:64],
                   sm[:, 48:56].rearrange("p (h o) -> p h o", o=1).to_broadcast([128, 8, 64]), ALU.mult, ["sm"],
                   ["B4", "B5", f"AO{s2}"])
                yield

            def front_indexer(n, do_topk, L):
                if do_topk:
                    nblk = (L + 511) // 512
                    items = [(kb, h) for kb in range(nblk) for h in range(8)]

                    def geom(kb):
                        w = min(512, L - kb * 512)
                        soff = 512 * (kb % 2)
                        return w, soff, banks(2, soff, soff + 512)

                    dbank = [(3, 0), (3, 512), (0, 0), (0, 512)]

                    def dots(i):
                        kb, h = items[i]
                        w, soff, sk = geom(kb)
                        e = h % 2
                        dpb, doff = dbank[i % 4]
                        dk = banks(dpb, doff, doff + 512)
                        MM(PB[dpb][:, doff:doff + w], iqT[64 * e:64 * e + 64, h // 2, :],
                           ikT2[64 * e:64 * e + 64, kb * 512:kb * 512 + w], True, True,
                           ["iqT"] + [f"ikT{t}" for t in range(kb * 4, min(n + 1, kb * 4 + 4))], dk)

                    def relu(i):
                        kb, h = items[i]
                        w, soff, sk = geom(kb)
                        dpb, doff = dbank[i % 4]
                        dk = banks(dpb, doff, doff + 512)
                        if i % 2 == 0:
                            ACT(Rb[i % 4][:, 0:w], PB[dpb][:, doff:doff + w], AF.Relu, [], dk + [f"Rb{i % 4}"])
                        else:
                            TS("dve", Rb[i % 4][:, 0:w], PB[dpb][:, doff:doff + w], 0.0, None, ALU.max, None, [],
                               dk + [f"Rb{i % 4}"])

                    def accum(i):
                        kb, h = items[i]
                        w, soff, sk = geom(kb)
                        MM(PB[2][:, soff:soff + w], sgnD[:, h, :], Rb[i % 4][:, 0:w], h == 0, h == 7,
                           ["sgnD", f"Rb{i % 4}"], sk)
                        if h == 7:
                            if kb == nblk - 1:
                                wd = w - 128
                                if wd > 0:
                                    CP("dve", score[:, kb * 512:kb * 512 + wd], PB[2][:, soff:soff + wd], [],
                                       sk + ["score"])
                                TT("dve", score[:, L - 128:L], PB[2][:, soff + wd:soff + w], cmaskq[:], ALU.add,
                                   ["cmaskq"], sk + ["score"])
                            else:
                                CP("dve", score[:, kb * 512:kb * 512 + 512], PB[2][:, soff:soff + 512], [],
                                   sk + ["score"])
                    npair = len(items) // 2
                    for pi in range(npair + 1):
                        if pi < npair:
                            dots(2 * pi)
                            dots(2 * pi + 1)
                            relu(2 * pi)
                            relu(2 * pi + 1)
                        if pi >= 1:
                            accum(2 * pi - 2)
                            accum(2 * pi - 1)
                        yield
                    yield

            def middle(n):
                if n < 2:
                    return
                s2 = n % 2
                L = (n + 1) * 128
                nm = NM[s2]
                nmk = f"NM{s2}"
                bk = ["bis"]
                RED(bis[:, 0:1], score[:, 0:L], ALU.max, ["score"], bk)
                RED(bis[:, 1:2], score[:, 0:L - 128], ALU.min, ["score"], bk)
                yield
                TT("dve", bis[:, 2:3], bis[:, 0:1], bis[:, 1:2], ALU.subtract, bk, bk)
                TS("dve", bis[:, 8:8 + NIT], pow2[:], bis[:, 2:3], None, ALU.mult, None, bk + ["pow2"], bk)
                TT("dve", bis[:, 3:4], bis[:, 1:2], bis[:, 8:9], ALU.add, bk, bk)
                for it in range(NIT):
                    TS("dve", nm[:, 0:L], score[:, 0:L], bis[:, 3:4], 0.0, ALU.is_ge, ALU.add, ["score"] + bk,
                       [nmk] + bk, accum_out=bis[:, 4:5])
                    TS("dve", bis[:, 5:6], bis[:, 4:5], TOPK - 0.5, bis[:, 8 + it:9 + it], ALU.is_ge, ALU.mult, bk, bk)
                    nxt = min(it + 1, NIT - 1)
                    STT(bis[:, 3:4], bis[:, 3:4], bis[:, 8 + nxt:9 + nxt], bis[:, 5:6], ALU.subtract, ALU.add, bk, bk)
                    yield
                TS("dve", nm[:, 0:L], score[:, 0:L], bis[:, 3:4], NEGM, ALU.is_lt, ALU.mult, ["score"] + bk, [nmk])
                yield

            def back(n):
                s2 = n % 3
                do_topk = n >= 2
                nm = NM[n % 2]
                nmk = f"NM{n % 2}"
                ql = qlatT[s2]
                qlk = f"qlatT{s2}"
                ao = AO[s2]
                aok = f"AO{s2}"
                items = [(hf, j) for hf in range(2) for j in range(n + 1)]

                def logits(i):
                    hf, j = items[i]
                    par = i % 2
                    p3 = i % 3
                    near = j >= n - 1
                    lo = PB[0][:, 512 * par:512 * par + 512]
                    lk = [f"B{par}"]
                    MM(lo, ckvT[:, j * 128:(j + 1) * 128], ql[:, 4 * hf:4 * hf + 4, :].rearrange("p h q -> p (h q)"),
                       True, not do_topk, [f"ckvT{j}", qlk], lk)
                    if do_topk:
                        MM(lo, nm[:, j * 128:(j + 1) * 128], ident4[:].rearrange("p r q -> p (r q)"), False, True,
                           [nmk, "ident4"], lk)
                    if near:
                        c = 0 if j == n else 1
                        lfd = LfS[:, 512 * par:512 * par + 512]
                        TT("dve", lfd.rearrange("p (h q) -> p h q", h=4), lo.rearrange("p (h q) -> p h q", h=4),
                           dsaBT[:, 4 * hf:4 * hf + 4, c, :], ALU.add, ["dsaBT"], lk + ["LfS"])
                        ACT(PTD[:, p3, :], lfd, AF.Exp, ["LfS"], [f"PTD{p3}"])
                    else:
                        ACT(PTD[:, p3, :], lo, AF.Exp, [], lk + [f"PTD{p3}"])

                def pv(i):
                    hf, j = items[i]
                    p3 = i % 3
                    MM(PB[1][:, 0:512], ckv[:, j, :], PTD[:, p3, :], j == 0, j == n, [f"ckv{j}", f"PTD{p3}"], ["B2"])
                    MM(PB[1][:, 512:1024], ones_b[:], PTD[:, p3, :], j == 0, j == n, ["ones_b", f"PTD{p3}"], ["B3"])
                    if j == n:
                        RECIP_ACT(rs[:], PB[1][:, 512:1024], rs[:], [], ["B3", "rs"])
                        TT("dve", olat[:, 512 * hf:512 * hf + 512], PB[1][:, 0:512], rs[:], ALU.mult, ["rs"],
                           ["B2", "olat"])
                for i in range(len(items) + 1):
                    if i < len(items):
                        logits(i)
                    if i >= 1:
                        pv(i - 1)
                    yield
                for h in range(8):
                    MM(PB[0][:, h * 64:(h + 1) * 64], olat[:, h * 128:(h + 1) * 128], wuv[:, h, :], True, True,
                       ["olat", "wuv"], ["B0"])
                ACT(ao[:, 512:1024], PB[0][:, 0:512], AF.Identity, [], ["B0", aok])
                yield
                for kc in range(KC):
                    MM(PB[0][:, kc * 128:(kc + 1) * 128], ao[:, kc * 128:(kc + 1) * 128], ident_b[:], True, True,
                       [aok, "ident_b"], banks(0, kc * 128, (kc + 1) * 128))
                ACT(aoT[:, 0:4, :].rearrange("p a b -> p (a b)"), PB[0][:, 0:512], AF.Identity, [], ["B0", "aoT"])
                CP("dve", aoT[:, 4:8, :].rearrange("p a b -> p (a b)"), PB[0][:, 512:1024], [], ["B1", "aoT"])
                yield
                for hf in range(2):
                    o = PB[1][:, 512 * hf:512 * hf + 512]
                    for kc in range(KC):
                        MM(o, aoT[:, kc, :], w_o_sb[:, kc, 512 * hf:512 * hf + 512], kc == 0, False,
                           ["aoT"] + wkeys("w_o_sb", kc, 512 * hf, 512 * hf + 512), banks(1, 512 * hf, 512 * hf + 512))
                    bias_mm(o, hl_o, 512 * hf, 512 * hf + 512, "hl_o", banks(1, 512 * hf, 512 * hf + 512))
                    yield
                y = Y[0]
                yk = "Y0"
                STT(y[:], XN[s2][:], ALPHA, PB[1][:], ALU.mult, ALU.add, [f"XN{s2}"], ["B2", "B3", yk])
                layernorm(y[:], [yk], y[:], yk, None, None, [], stB, "stB", affine=False)
                DMA("sp", x1_d.ap()[n * 128:(n + 1) * 128, :], y[:], [yk], [f"x1d{n}"])
                yield

            def run_interleaved(gens):
                gens = [g for g in gens if g is not None]
                while gens:
                    alive = []
                    for g in gens:
                        try:
                            next(g)
                            alive.append(g)
                        except StopIteration:
                            pass
                    gens = alive

            nA = n_tiles if dbg >= 2 else 0
            if nA > 0:
                run_interleaved([front(0, 0)])
                run_interleaved([front(0, 1)])
                run_interleaved([front(0, 2)])
            def chain(*gens):
                for g in gens:
                    if g is not None:
                        yield from g

            def run_weighted(gm, gmain, ratio):
                alive_m, alive_x = gm is not None, gmain is not None
                while alive_m or alive_x:
                    if alive_m:
                        try:
                            next(gm)
                        except StopIteration:
                            alive_m = False
                    for _ in range(ratio if alive_m else 10 ** 6):
                        if not alive_x:
                            break
                        try:
                            next(gmain)
                        except StopIteration:
                            alive_x = False

            for t in range(nA + 1):
                gm = middle(t) if (t < nA and t >= 2) else None
                gb = back(t - 1) if t >= 1 else None
                gf = front(t + 1, 1) if t + 1 < nA else None
                gh = front(t + 1, 0) if t + 1 < nA else None
                len_m = NIT + 2
                len_x = (2 * t + 10 if t >= 1 else 0) + (22 if gf is not None else 0)
                ratio = max(1, -(-len_x // len_m))
                run_weighted(gm, chain(gh, gb, gf), ratio)
                if t + 1 < nA:
                    run_interleaved([front(t + 1, 2)])
            P.barrier()
            P.emit()

        if dbg > 8:
            with ExitStack() as sB:
                wq_sb = load_weight_bf16(sB, "wq_sb", "xa_wq", D)
                xwo_sb = load_weight_bf16(sB, "xwo_sb", "xa_wo", D)
                g1_bc = load_bcast(sB, "g1_bc", "ln1_g", D)
                b1_bc = load_bcast(sB, "b1_bc", "ln1_b", D)
                g2_bc = load_bcast(sB, "g2_bc", "ln2_g", D)
                b2_bc = load_bcast(sB, "b2_bc", "ln2_b", D)
                brow_v = sbt(sB, "brow_v", [1, D], F32)
                DMA("sp", brow_v[:], dr["xa_bkv"].ap()[D:2 * D].rearrange("(o n) -> o n", o=1), (), ["brow_v"])
                bq16 = sbt(sB, "bq16", [128, 8], F32)
                DMA("sp", bq16[:], dr["xa_bq"].ap().rearrange("(j p) -> p j", p=128), (), ["bq16"],
                    allow_slow_non_contiguous=True)
                TS("dve", bq16[:], bq16[:], 1.0 / 16.0, None, ALU.mult, None, ["bq16"], ["bq16"])
                bkc = sbt(sB, "bkc", [128, 8], F32)
                DMA("sp", bkc[:], dr["xa_bkv"].ap()[0:D].rearrange("(j p) -> p j", p=128), (), ["bkc"],
                    allow_slow_non_contiguous=True)
                hl_xo = sbt(sB, "hl_xo", [2, D], BF16)
                kmT = sbt(sB, "kmT", [128, 8, 256], BF16)
                vm = sbt(sB, "vm", [128, 2, D], BF16)
                with ExitStack() as sS:
                    make_bias_hl(hl_xo, sS, "hl_xo", [(0, D, dr["xa_bo"].ap().rearrange("(o n) -> o n", o=1))], D)
                    wk_sb = load_weight_bf16(sS, "wk_sb", "xa_wkv", 2 * D, 0, D)
                    wv_sb = load_weight_bf16(sS, "wv_sb", "xa_wkv", 2 * D, D, 2 * D)
                    memb = sbt(sS, "memb", [128, 2, D], BF16)
                    memT = sbt(sS, "memT", [128, KC, 256], BF16)
                    DMA("pool", memb[:], dr["mem"].ap().rearrange("(c p) d -> p c d", p=128), (), ["memb"])
                    for mc in range(2):
                        for kc in range(KC):
                            MM(PB[mc][:, kc * 128:(kc + 1) * 128], memb[:, mc, kc * 128:(kc + 1) * 128], ident_b[:],
                               True, True, ["memb", "ident_b"], banks(mc, kc * 128, (kc + 1) * 128))
                        CP("dve", memT[:, :, mc * 128:(mc + 1) * 128], PB[mc][:].rearrange("p (k m) -> p k m", k=8), [],
                           banks(mc) + ["memT"])
                    for fc in range(8):
                        pb, off = 2 + fc // 4, (fc % 4) * 256
                        for kc in range(KC):
                            MM(PB[pb][:, off:off + 256], wk_sb[:, kc, fc * 128:(fc + 1) * 128], memT[:, kc, :], kc == 0,
                               kc == KC - 1, wkeys("wk_sb", kc, fc * 128, (fc + 1) * 128) + ["memT"],
                               banks(pb, off, off + 256))
                        ACT(kmT[:, fc, :], PB[pb][:, off:off + 256], AF.Identity, ["bkc"],
                            banks(pb, off, off + 256) + ["kmT"], bias=bkc[:, fc:fc + 1])
                    for mc in range(2):
                        for hf in range(2):
                            o = PB[mc][:, 512 * hf:512 * hf + 512]
                            for kc in range(KC):
                                MM(o, memT[:, kc, mc * 128:(mc + 1) * 128], wv_sb[:, kc, 512 * hf:512 * hf + 512],
                                   kc == 0, False, ["memT"] + wkeys("wv_sb", kc, 512 * hf, 512 * hf + 512),
                                   banks(mc, 512 * hf, 512 * hf + 512))
                            MM(o, ones_f[0:1, :], brow_v[:, 512 * hf:512 * hf + 512], False, True, ["ones_f", "brow_v"],
                               banks(mc, 512 * hf, 512 * hf + 512))
                        CP("dve", vm[:, mc, :], PB[mc][:], [], banks(mc) + ["vm"])
                    P.barrier()
                    P.emit()

                WB = 4
                x1in = [sbt(sB, f"x1in{i}", [128, D], F32) for i in range(WB)]
                Yb = [sbt(sB, f"Yb{i}", [128, D], F32) for i in range(WB)]
                bfB = [sbt(sB, f"bfB{i}", [128, D], BF16) for i in range(WB)]
                x1T = [sbt(sB, f"x1T{i}", [128, KC, 128], BF16) for i in range(WB)]
                xqT = [sbt(sB, f"xqT{i}", [128, KC, 128], BF16) for i in range(WB)]
                PTx = [sbt(sB, f"PTx{i}", [128, 1024], BF16) for i in range(WB)]
                rbc = [sbt(sB, f"rbc{i}", [128, 512], F32) for i in range(WB)]
                caoT = [sbt(sB, f"caoT{i}", [128, KC, 128], BF16) for i in range(WB)]
                stb = [sbt(sB, f"stb{i}", [128, 24], F32) for i in range(WB)]
                print("pass B sbuf bytes remaining", nc.sbuf_bytes_remaining)

                def tileB(n):
                    p = n % WB
                    pa = p
                    xi = x1in[p]
                    xk = f"x1in{p}"
                    DMA("sp", xi[:], x1_d.ap()[n * 128:(n + 1) * 128, :], [f"x1d{n}"], [xk])
                    TT("dve", xi[:], xi[:], g1_bc[:], ALU.mult, [xk, "g1_bc"], [xk])
                    TT("dve", xi[:], xi[:], b1_bc[:], ALU.add, [xk, "b1_bc"], [xk])
                    yield
                    to_feature_major(xi[:], xk, bfB[p][:], f"bfB{p}", x1T[p], f"x1T{p}", pa, ("act", "dve"),
                                     cast_eng="dve")
                    yield
                    for fc in range(8):
                        o = PB[pa][:, fc * 128:(fc + 1) * 128]
                        for kc in range(KC):
                            MM(o, wq_sb[:, kc, fc * 128:(fc + 1) * 128], x1T[p][:, kc, :], kc == 0, kc == KC - 1,
                               wkeys("wq_sb", kc, fc * 128, (fc + 1) * 128) + [f"x1T{p}"],
                               banks(pa, fc * 128, (fc + 1) * 128))
                        if fc % 2 == 1:
                            yield
                    for fc in range(8):
                        ACT(xqT[p][:, fc, :], PB[pa][:, fc * 128:(fc + 1) * 128], AF.Identity, ["bq16"],
                            banks(pa, fc * 128, (fc + 1) * 128) + [f"xqT{p}"], bias=bq16[:, fc:fc + 1],
                            scale=1.0 / 16.0)
                    yield
                    for mc in range(2):
                        for h in range(4):
                            o = PB[pa][:, (mc * 4 + h) * 128:(mc * 4 + h + 1) * 128]
                            for kk in range(2):
                                MM(o, kmT[:, 2 * h + kk, mc * 128:(mc + 1) * 128], xqT[p][:, 2 * h + kk, :], kk == 0,
                                   kk == 1, ["kmT", f"xqT{p}"], banks(pa, (mc * 4 + h) * 128, (mc * 4 + h + 1) * 128))
                        yield
                    ACT(PTx[p][:], PB[pa][:], AF.Exp, [], banks(pa) + [f"PTx{p}"])
                    yield
                    for mc in range(2):
                        MM(PB[pa][:, 0:512], ones_b[:], PTx[p][:, mc * 512:(mc + 1) * 512], mc == 0, mc == 1,
                           ["ones_b", f"PTx{p}"], banks(pa, 0, 512))
                    RECIP_ACT(rbc[p][:], PB[pa][:, 0:512], rbc[p][:], [], banks(pa, 0, 512) + [f"rbc{p}"])
                    yield
                    for fc in range(8):
                        h = fc // 2
                        o = PB[pa][:, fc * 128:(fc + 1) * 128]
                        for mc in range(2):
                            MM(o, vm[:, mc, fc * 128:(fc + 1) * 128],
                               PTx[p][:, (mc * 4 + h) * 128:(mc * 4 + h + 1) * 128], mc == 0, mc == 1,
                               ["vm", f"PTx{p}"], banks(pa, fc * 128, (fc + 1) * 128))
                        if fc % 4 == 3:
                            yield
                    TT("dve", caoT[p][:].rearrange("p (h t) q -> p h t q", h=4),
                       PB[pa][:].rearrange("p (h t q) -> p h t q", h=4, t=2),
                       rbc[p][:].rearrange("p (h o q) -> p h o q", h=4, o=1).to_broadcast([128, 4, 2, 128]), ALU.mult,
                       [f"rbc{p}"], banks(pa) + [f"caoT{p}"])
                    yield
                    for hf in range(2):
                        o = PB[pa][:, 512 * hf:512 * hf + 512]
                        for kc in range(KC):
                            MM(o, caoT[p][:, kc, :], xwo_sb[:, kc, 512 * hf:512 * hf + 512], kc == 0, False,
                               [f"caoT{p}"] + wkeys("xwo_sb", kc, 512 * hf, 512 * hf + 512),
                               banks(pa, 512 * hf, 512 * hf + 512))
                        bias_mm(o, hl_xo, 512 * hf, 512 * hf + 512, "hl_xo", banks(pa, 512 * hf, 512 * hf + 512))
                        yield
                    STT(Yb[p][:], xi[:], ALPHA, PB[pa][:], ALU.mult, ALU.add, [xk], banks(pa) + [f"Yb{p}"])
                    yield
                    layernorm(Yb[p][:], [f"Yb{p}"], Yb[p][:], f"Yb{p}", g2_bc[:], b2_bc[:], ["g2_bc", "b2_bc"], stb[p],
                              f"stb{p}", aff_eng="dve")
                    DMA("sp", x2_d.ap()[n * 128:(n + 1) * 128, :], Yb[p][:], [f"Yb{p}"], [f"x2d{n}"])
                    yield

                run_pipeline(tileB, n_tiles, WB, 6)
                P.barrier()
                P.emit()

        if dbg > 9:
            with ExitStack() as sC:
                wup = sbt(sC, "wup", [128, KC, DFF], BF16)
                wdn = sbt(sC, "wdn", [128, 32, D], BF16)
                g3_bc = load_bcast(sC, "g3_bc", "ln3_g", D)
                b3_bc = load_bcast(sC, "b3_bc", "ln3_b", D)
                hl_d = sbt(sC, "hl_d", [2, D], BF16)
                bupc = sbt(sC, "bupc", [128, 32], F32)
                DMA("sp", bupc[:], dr["b_up"].ap().rearrange("(j p) -> p j", p=128), (), ["bupc"],
                    allow_slow_non_contiguous=True)
                with ExitStack() as sS:
                    make_bias_hl(hl_d, sS, "hl_d", [(0, D, dr["b_down"].ap().rearrange("(o n) -> o n", o=1))], D)
                    P.barrier()
                    P.emit()
                load_weight_into(wup, "wup", "w_up", DFF, D)
                load_weight_into(wdn, "wdn", "w_down", D, DFF)
                G = 2
                x2in = [sbt(sC, f"x2in{i}", [128, D], F32) for i in range(2 * G)]
                bfC = sbt(sC, "bfC", [128, D], BF16)
                x2T = [sbt(sC, f"x2T{i}", [128, KC, G * 128], BF16) for i in range(2)]
                x2Tt = sbt(sC, "x2Tt", [128, KC, 128], BF16)
                rT = sbt(sC, "rT", [128, 2, G * 128], BF16)
                gT = sbt(sC, "gT", [128, 32, G * 128], BF16)
                Yc = [sbt(sC, f"Yc{i}", [128, D], F32) for i in range(2)]
                stc = sbt(sC, "stc", [128, 24], F32)
                print("pass C sbuf bytes remaining", nc.sbuf_bytes_remaining)
                ngrp = (n_tiles + G - 1) // G

                def prepC(gi):
                    gp = gi % 2
                    tiles = list(range(gi * G, min(n_tiles, (gi + 1) * G)))
                    for ti, n in enumerate(tiles):
                        bi = gp * G + ti
                        xi = x2in[bi]
                        xk = f"x2in{bi}"
                        DMA("sp", xi[:], x2_d.ap()[n * 128:(n + 1) * 128, :], [f"x2d{n}"], [xk])
                        yield
                        to_feature_major(xi[:], xk, bfC[:], "bfC", x2T[gp][:, :, ti * 128:(ti + 1) * 128],
                                         f"x2T{gp}", 0, ("act", "dve"), cast_eng="dve")
                        yield

                def mlpC(gi):
                    gp = gi % 2
                    tiles = list(range(gi * G, min(n_tiles, (gi + 1) * G)))
                    ntk = len(tiles) * 128
                    for fc in range(32):
                        off = 512 * (fc % 2)
                        o = PB[1][:, off:off + ntk]
                        for kc in range(KC):
                            MM(o, wup[:, kc, fc * 128:(fc + 1) * 128], x2T[gp][:, kc, 0:ntk], kc == 0, kc == KC - 1,
                               wkeys("wup", kc, fc * 128, (fc + 1) * 128) + [f"x2T{gp}"], banks(1, off, off + ntk))
                        ACT(rT[:, fc % 2, 0:ntk], o, AF.Relu, ["bupc"], banks(1, off, off + ntk) + [f"rT{fc % 2}"],
                            bias=bupc[:, fc:fc + 1])
                        TT("dve" if fc % 4 != 3 else "pool", gT[:, fc, 0:ntk], rT[:, fc % 2, 0:ntk], rT[:, fc % 2, 0:ntk],
                           ALU.mult, [f"rT{fc % 2}"], [f"gT{fc}"])
                        if fc % 2 == 1:
                            yield
                    for ti, n in enumerate(tiles):
                        bi = gp * G + ti
                        xi = x2in[bi]
                        xk = f"x2in{bi}"
                        yc = Yc[n % 2]
                        yk = f"Yc{n % 2}"
                        dp = 3 if ti == 0 else 2
                        for hf in range(2):
                            o = PB[dp][:, 512 * hf:512 * hf + 512]
                            for fc in range(32):
                                MM(o, gT[:, fc, ti * 128:(ti + 1) * 128], wdn[:, fc, 512 * hf:512 * hf + 512], fc == 0,
                                   False, [f"gT{fc}"] + wkeys("wdn", fc, 512 * hf, 512 * hf + 512),
                                   banks(dp, 512 * hf, 512 * hf + 512))
                                if fc % 8 == 7:
                                    yield
                            bias_mm(o, hl_d, 512 * hf, 512 * hf + 512, "hl_d", banks(dp, 512 * hf, 512 * hf + 512))
                        STT(yc[:], xi[:], ALPHA, PB[dp][:], ALU.mult, ALU.add, [xk], banks(dp) + [yk])
                        layernorm(yc[:], [yk], yc[:], yk, g3_bc[:], b3_bc[:], ["g3_bc", "b3_bc"], stc, "stc",
                                  aff_eng="dve")
                        DMA("sp", y_d.ap()[n * 128:(n + 1) * 128, :], yc[:], [yk], [f"yd{n}"])
                        yield

                def run_il(gens):
                    gens = list(gens)
                    while gens:
                        alive = []
                        for g in gens:
                            try:
                                next(g)
                                alive.append(g)
                            except StopIteration:
                                pass
                        gens = alive

                if ngrp > 0:
                    run_il([prepC(0)])
                for gi in range(ngrp):
                    gs = [mlpC(gi)]
                    if gi + 1 < ngrp:
                        gs.append(prepC(gi + 1))
                    run_il(gs)
                P.op("sp", None, [f"yd{n}" for n in range(n_tiles)], ())
                P.emit()
    return nc


_NC_CACHE = {}


def kernel(**inputs):
    consts = _constants()
    if "nc" not in _NC_CACHE:
        _NC_CACHE["nc"] = build_nc()
    nc = _NC_CACHE["nc"]
    in_maps = []
    for c in range(N_CORES):
        m = {}
        for k in PARAM_SHAPES:
            a = np.asarray(inputs[k], dtype=np.float32)
            if k == "x" or k == "mem":
                a = a[c]
            elif k != "rel_bias" and k not in ("ln_emb_g", "ln_emb_b"):
                a = a[0]
            m[k] = np.ascontiguousarray(a).reshape(PARAM_SHAPES[k])
        m.update(consts)
        in_maps.append(m)
    res = run_bass_kernel_spmd(nc, in_maps, core_ids=list(range(N_CORES)))
    return np.stack([np.asarray(r["y"], dtype=np.float32) for r in res.results], axis=0)
```

```python
import math
from contextlib import ExitStack

import numpy as np
import concourse.bass as bass
import concourse.mybir as mybir
from concourse.bass_utils import run_bass_kernel_spmd

F32 = mybir.dt.float32
BF16 = mybir.dt.bfloat16
AF = mybir.ActivationFunctionType
ALU = mybir.AluOpType
AX = mybir.AxisListType

T = 4096
D = 1024
NT = 32
KC = 8
DFF = 4096
DIN = 1992
ALPHA = 2.0 ** 0.25
LN_EPS = 1e-5
NEGM = -30000.0
NIT = 14
TOPK = 256
N_CORES = 8

ENGS = ("pe", "act", "dve", "pool", "sp")
EPOCH = 24000
NDMA_SEMS = 12


class Op:
    __slots__ = ("eng", "fn", "deps", "dma", "idx", "sig", "know", "has_dependents")

    def __init__(self, eng, fn, dma):
        self.eng = eng
        self.fn = fn
        self.dma = dma
        self.deps = {}
        self.sig = None
        self.know = None
        self.has_dependents = False


class Prog:
    def __init__(self, nc, ctx):
        self.nc = nc
        self.ctx = ctx
        self.ops = []
        self.res = {}
        self.engobj = {"pe": nc.tensor, "act": nc.scalar, "dve": nc.vector, "pool": nc.gpsimd, "sp": nc.sync}
        self.sems = {}
        self.ticks = {e: 0 for e in ENGS}
        self.dma_count = {e: 0 for e in ENGS}
        self.dma_last = {}
        self.know = {e: {} for e in ENGS}
        self.emitted = 0
        self.last_compute = {}
        self.dmas_since_barrier = []
        self.barrier_idx = None
        self.barrier_seen = set()

    def op(self, eng, fn, reads=(), writes=(), dma=False):
        o = Op(eng, fn, dma)
        o.idx = len(self.ops)
        deps = {}

        def add(d, raw):
            if d is None:
                return
            do = self.ops[d]
            if (not dma) and (not do.dma) and do.eng == eng:
                if not raw or eng == "pe":
                    return
            deps[d] = True

        for k in reads:
            r = self.res.setdefault(k, [None, []])
            add(r[0], True)
        for k in writes:
            r = self.res.setdefault(k, [None, []])
            add(r[0], False)
            for rd in r[1]:
                add(rd, False)
        for k in reads:
            self.res[k][1].append(o.idx)
        for k in writes:
            r = self.res[k]
            r[0] = o.idx
            r[1] = []
        if self.barrier_idx is not None and eng not in self.barrier_seen:
            deps[self.barrier_idx] = True
            self.barrier_seen.add(eng)
        o.deps = deps
        for d in deps:
            self.ops[d].has_dependents = True
        self.ops.append(o)
        if dma:
            self.dmas_since_barrier.append(o.idx)
        elif fn is not None:
            self.last_compute[eng] = o.idx
        return o

    def barrier(self):
        o = Op("sp", lambda e: e.nop(), False)
        o.idx = len(self.ops)
        deps = {}
        for e, i in self.last_compute.items():
            deps[i] = True
        for i in self.dmas_since_barrier:
            deps[i] = True
        if self.barrier_idx is not None:
            deps[self.barrier_idx] = True
        o.deps = deps
        for d in deps:
            self.ops[d].has_dependents = True
        o.has_dependents = True
        self.ops.append(o)
        self.barrier_idx = o.idx
        self.barrier_seen = {"sp"}
        self.dmas_since_barrier = []
        self.last_compute = {"sp": o.idx}
        self.res = {}

    def _sem(self, name):
        if name not in self.sems:
            self.sems[name] = self.ctx.enter_context(self.nc.semaphore(name))
        return self.sems[name]

    def emit(self):
        def merge(dst, src):
            for k, v in src.items():
                if dst.get(k, 0) < v:
                    dst[k] = v

        for o in self.ops[self.emitted:]:
            eng = o.eng
            eo = self.engobj[eng]
            deps = dict(o.deps)
            if o.dma:
                slot = self.dma_count[eng] % NDMA_SEMS
                prev = self.dma_last.get((eng, slot))
                if prev is not None:
                    deps[prev] = True
            kn = self.know[eng]
            need = {}
            for d in deps:
                src, val = self.ops[d].sig
                if kn.get(src, 0) < val and need.get(src, 0) < val:
                    need[src] = val
            for src, val in need.items():
                eo.wait_ge(self._sem(src), val)
            for d in deps:
                do = self.ops[d]
                src, val = do.sig
                if kn.get(src, 0) < val:
                    kn[src] = val
                merge(kn, do.know)
            inst = o.fn(eo) if o.fn is not None else None
            if o.dma:
                slot = self.dma_count[eng] % NDMA_SEMS
                n = self.dma_count[eng] // NDMA_SEMS + 1
                name = f"d_{eng}_{slot}"
                inst.then_inc(self._sem(name), 16)
                o.sig = (name, 16 * n)
                self.dma_last[(eng, slot)] = o.idx
                self.dma_count[eng] += 1
            elif o.has_dependents and inst is not None:
                self.ticks[eng] += 1
                ep = self.ticks[eng] // EPOCH
                val = self.ticks[eng] - ep * EPOCH
                if val == 0:
                    self.ticks[eng] += 1
                    val = 1
                name = f"c_{eng}_{ep}"
                inst.then_inc(self._sem(name), 1)
                o.sig = (name, val)
            else:
                o.sig = ("none", 0)
            o.know = dict(kn)
            o.fn = None
        self.emitted = len(self.ops)


def _rel_bucket_np(n):
    n = np.maximum(n, 0)
    me = 16
    nf = np.maximum(n, 1).astype(np.float32)
    large = me + (np.log(nf / np.float32(me)) / np.float32(math.log(128 / 16)) * np.float32(16)).astype(np.int32)
    large = np.minimum(large, 31)
    return np.where(n < me, n, large)


def _constants():
    c = {}
    c["c_ident"] = np.eye(128, dtype=np.float32)
    c["c_exch"] = np.ascontiguousarray(np.eye(128, dtype=np.float32)[::-1])
    ohv = np.zeros((32, 2, 384), np.float32)
    rel = np.arange(384) - 127
    bk = _rel_bucket_np(rel)
    for i in range(384):
        if rel[i] >= 0:
            ohv[bk[i], 0, i] = 1.0
            ohv[bk[i], 1, i] += 1.0
            ohv[31, 1, i] -= 1.0
    c["c_ohv"] = ohv
    p = np.arange(128)[:, None]
    q = np.arange(128)[None, :]
    m = np.zeros((128, 2, 2, 128), np.float32)
    cur = np.where(q >= p, 0.0, NEGM)
    m[:, 0, 0, :] = cur
    m[:, 1, 0, :] = cur
    m[:, 0, 1, :] = np.where(q < p, 0.0, NEGM)
    m[:, 1, 1, :] = 0.0
    c["c_maskT"] = m
    qq = np.arange(128)[:, None]
    ss = np.arange(128)[None, :]
    c["c_cmaskq"] = np.where(ss <= qq, 0.0, -1e30).astype(np.float32)
    c["c_pow2"] = np.tile((0.5 ** (np.arange(NIT) + 1)).astype(np.float32)[None, :], (128, 1))
    return c


CONST_SHAPES = {
    "c_ident": [128, 128], "c_exch": [128, 128], "c_ohv": [32, 2, 384], "c_maskT": [128, 2, 2, 128],
    "c_cmaskq": [128, 128], "c_pow2": [128, NIT],
}

PARAM_SHAPES = {
    "x": [T, D], "mem": [256, D], "ln_emb_g": [D], "ln_emb_b": [D], "w_in": [D, DIN], "b_in": [DIN],
    "swa_sinks": [8], "dsa_kv_norm_g": [128], "dsa_w_uk": [8, 128, 64], "dsa_w_uv": [8, 128, 64],
    "idx_k_ln_g": [64], "idx_k_ln_b": [64], "rel_bias": [32, 16], "w_o": [D, D], "b_o": [D],
    "ln1_g": [D], "ln1_b": [D], "xa_wq": [D, D], "xa_bq": [D], "xa_wkv": [D, 2 * D], "xa_bkv": [2 * D],
    "xa_wo": [D, D], "xa_bo": [D], "ln2_g": [D], "ln2_b": [D], "w_up": [D, DFF], "b_up": [DFF],
    "w_down": [DFF, D], "b_down": [D], "ln3_g": [D], "ln3_b": [D],
}


class _Done(Exception):
    pass


def build_nc(n_tiles=NT, dbg=99):
    nc = bass.Bass("TRN2", target_bir_lowering=False)
    dr = {}
    for k, shp in PARAM_SHAPES.items():
        dr[k] = nc.dram_tensor(k, shp, F32, kind="ExternalInput")
    for k, shp in CONST_SHAPES.items():
        dr[k] = nc.dram_tensor(k, shp, F32, kind="ExternalInput")
    y_d = nc.dram_tensor("y", [T, D], F32, kind="ExternalOutput")
    x1_d = nc.dram_tensor("x1_scr", [T, D], F32)
    x2_d = nc.dram_tensor("x2_scr", [T, D], F32)
    vec_d = nc.dram_tensor("vec_scr", [16, 384], F32)

    def bcast_rows(name, n, off=0, parts=128):
        return bass.AP(dr[name], off, [[0, parts], [1, n]])

    ctx = ExitStack()
    with ctx:
        P = Prog(nc, ctx)

        def sbt(stack, name, shape, dt):
            return stack.enter_context(nc.sbuf_tensor(name, shape, dt))

        PB = [ctx.enter_context(nc.psum_tensor(f"PB{i}", [128, 1024], F32)) for i in range(4)]

        def banks(i, lo=0, hi=1024):
            ks = []
            if lo < 512:
                ks.append(f"B{2 * i}")
            if hi > 512:
                ks.append(f"B{2 * i + 1}")
            return ks

        def MM(out, lhsT, rhs, start, stop, reads, writes):
            P.op("pe", lambda e: e.matmul(out, lhsT=lhsT, rhs=rhs, start=start, stop=stop), reads, writes)

        def ACT(out, in_, func, reads, writes, bias=None, scale=1.0, accum_out=None):
            kw = {}
            if bias is not None:
                kw["bias"] = bias
            if accum_out is not None:
                kw["accum_out"] = accum_out
            P.op("act", lambda e: e.activation(out=out, in_=in_, func=func, scale=scale, **kw), reads, writes)

        def TS(eng, out, in0, s1, s2, op0, op1, reads, writes, accum_out=None):
            if accum_out is not None:
                P.op(eng, lambda e: e.tensor_scalar(out=out, in0=in0, scalar1=s1, scalar2=s2, op0=op0, op1=op1,
                                                    accum_out=accum_out), reads, writes)
            elif op1 is None:
                P.op(eng, lambda e: e.tensor_scalar(out=out, in0=in0, scalar1=s1, scalar2=None, op0=op0), reads, writes)
            else:
                P.op(eng, lambda e: e.tensor_scalar(out=out, in0=in0, scalar1=s1, scalar2=s2, op0=op0, op1=op1),
                     reads, writes)

        def TT(eng, out, in0, in1, op, reads, writes):
            P.op(eng, lambda e: e.tensor_tensor(out=out, in0=in0, in1=in1, op=op), reads, writes)

        def STT(out, in0, scalar, in1, op0, op1, reads, writes):
            P.op("dve", lambda e: e.scalar_tensor_tensor(out=out, in0=in0, scalar=scalar, in1=in1, op0=op0, op1=op1),
                 reads, writes)

        def CP(eng, out, in_, reads, writes):
            P.op(eng, lambda e: e.tensor_copy(out=out, in_=in_), reads, writes)

        def DMA(q, out, in_, reads, writes, **kw):
            P.op(q, lambda e: e.dma_start(out=out, in_=in_, **kw), reads, writes, dma=True)

        def BNS(out, in_, reads, writes):
            P.op("dve", lambda e: e.bn_stats(out=out, in_=in_), reads, writes)

        def BNA(out, in_, reads, writes):
            P.op("dve", lambda e: e.bn_aggr(out=out, in_=in_), reads, writes)

        def RED(out, in_, op, reads, writes):
            P.op("dve", lambda e: e.tensor_reduce(out=out, in_=in_, axis=AX.X, op=op), reads, writes)

        def RECIP(out, in_, reads, writes):
            P.op("dve", lambda e: e.reciprocal(out=out, in_=in_), reads, writes)

        def RECIP_ACT(out, in_, tmp, reads, writes):
            ACT(tmp, in_, AF.Ln, reads, writes)
            ACT(out, tmp, AF.Exp, [], writes, scale=-1.0)

        def MEMSET(eng, ap, val, writes):
            P.op(eng, lambda e: e.memset(ap, val), (), writes)

        def run_pipeline(make, count, width, stagger):
            active = []
            nxt = 0
            since = stagger
            while nxt < count or active:
                if nxt < count and len(active) < width and (since >= stagger or not active):
                    active.append(make(nxt))
                    nxt += 1
                    since = 0
                alive = []
                for g in active:
                    try:
                        next(g)
                        alive.append(g)
                    except StopIteration:
                        pass
                active = alive
                since += 1

        ident_f = sbt(ctx, "ident_f", [128, 128], F32)
        ident_b = sbt(ctx, "ident_b", [128, 128], BF16)
        ident4 = sbt(ctx, "ident4", [128, 4, 128], BF16)
        ones_b = sbt(ctx, "ones_b", [128, 128], BF16)
        ones_f = sbt(ctx, "ones_f", [64, 128], F32)
        eps_t = sbt(ctx, "eps_t", [128, 1], F32)
        DMA("sp", ident_f[:], dr["c_ident"].ap(), (), ["ident_f"])
        CP("dve", ident_b[:], ident_f[:], ["ident_f"], ["ident_b"])
        for r in range(4):
            CP("dve", ident4[:, r, :], ident_f[:], ["ident_f"], ["ident4"])
        MEMSET("dve", ones_b[:], 1.0, ["ones_b"])
        MEMSET("dve", ones_f[:], 1.0, ["ones_f"])
        MEMSET("dve", eps_t[:], LN_EPS, ["eps_t"])

        def rstd_from_var(var_ap, out_ap, tmp_ap, rkeys, wkeys, scale=1.0):
            ACT(tmp_ap, var_ap, AF.Ln, rkeys + ["eps_t"], wkeys, bias=eps_t[:], scale=scale)
            ACT(out_ap, tmp_ap, AF.Exp, wkeys, wkeys, scale=-0.5)

        def layernorm(src, src_keys, dst, dst_key, g_bc, b_bc, gb_keys, st, st_key, affine=True, aff_eng="pool"):
            wk = [st_key]
            BNS(st[:, 0:6], src[:, 0:512], src_keys, wk)
            BNS(st[:, 6:12], src[:, 512:1024], src_keys + wk, wk)
            BNA(st[:, 12:14], st[:, 0:12], wk, wk)
            rstd_from_var(st[:, 13:14], st[:, 15:16], st[:, 14:15], wk, wk)
            TS("dve", dst, src, st[:, 12:13], st[:, 15:16], ALU.subtract, ALU.mult, src_keys + wk, [dst_key])
            if affine:
                TT(aff_eng, dst, dst, g_bc, ALU.mult, [dst_key] + gb_keys, [dst_key])
                TT(aff_eng, dst, dst, b_bc, ALU.add, [dst_key] + gb_keys, [dst_key])

        def to_feature_major(src, src_key, bf_tmp, bf_key, dstT, dstT_key, pb, evac, cast_eng="act"):
            if cast_eng is None:
                pass
            elif cast_eng == "act":
                ACT(bf_tmp, src, AF.Identity, [src_key], [bf_key])
            else:
                CP(cast_eng, bf_tmp, src, [src_key], [bf_key])
            for kc in range(KC):
                MM(PB[pb][:, kc * 128:(kc + 1) * 128], bf_tmp[:, kc * 128:(kc + 1) * 128], ident_b[:], True, True,
                   [bf_key, "ident_b"], banks(pb, kc * 128, (kc + 1) * 128))
            for hf in range(2):
                eng = evac[hf]
                o = dstT[:, 4 * hf:4 * hf + 4, :]
                i = PB[pb][:, 512 * hf:512 * hf + 512].rearrange("p (a b) -> p a b", a=4)
                if eng == "act":
                    ACT(o, i, AF.Identity, [], banks(pb, 512 * hf, 512 * hf + 512) + [dstT_key])
                else:
                    CP("dve", o, i, [], banks(pb, 512 * hf, 512 * hf + 512) + [dstT_key])

        def make_bias_hl(hl, tstack, name, pieces, n):
            f = sbt(tstack, name + "_f", [1, n], F32)
            hb = sbt(tstack, name + "_h", [1, n], BF16)
            lb = sbt(tstack, name + "_l", [1, n], BF16)
            for (d0, d1, src) in pieces:
                DMA("sp", f[:, d0:d1], src, (), [name + "_f"])
            CP("dve", hb[:], f[:], [name + "_f"], [name + "_h"])
            TT("dve", f[:], f[:], hb[:], ALU.subtract, [name + "_f", name + "_h"], [name + "_f"])
            CP("dve", lb[:], f[:], [name + "_f"], [name + "_l"])
            DMA("sp", hl[0:1, :], hb[:], [name + "_h"], [name])
            DMA("sp", hl[1:2, :], lb[:], [name + "_l"], [name])
            return hl

        def bias_mm(o, hl, lo, hi, key, bk):
            MM(o, ones_b[0:2, :], hl[0:2, lo:hi], False, True, ["ones_b", key], bk)

        def load_bcast(stack, name, src_name, n, off=0):
            t = sbt(stack, name, [128, n], F32)
            DMA("sp", t[:], bcast_rows(src_name, n, off), (), [name])
            return t

        def load_weight_bf16(stack, name, src_name, ncols, col_lo=0, col_hi=None, rows=D):
            if col_hi is None:
                col_hi = ncols
            nk = rows // 128
            w = col_hi - col_lo
            t = sbt(stack, name, [128, nk, w], BF16)
            src = dr[src_name].ap().rearrange("(k p) n -> p k n", p=128)
            for c0 in range(0, w, 1024):
                c1 = min(w, c0 + 1024)
                for k0 in range(0, nk, 4):
                    k1 = min(nk, k0 + 4)
                    DMA("pool", t[:, k0:k1, c0:c1], src[:, k0:k1, col_lo + c0:col_lo + c1], (),
                        [f"{name}:{k0 // 4}:{c0 // 1024}"])
            return t

        def load_weight_into(t, name, src_name, w, rows):
            nk = rows // 128
            src = dr[src_name].ap().rearrange("(k p) n -> p k n", p=128)
            for k0 in range(0, nk, 4):
                k1 = min(nk, k0 + 4)
                for c0 in range(0, w, 1024):
                    c1 = min(w, c0 + 1024)
                    DMA("pool", t[:, k0:k1, c0:c1], src[:, k0:k1, c0:c1], (), [f"{name}:{k0 // 4}:{c0 // 1024}"])

        def wkeys(name, kc, lo, hi):
            return [f"{name}:{kc // 4}:{c}" for c in range(lo // 1024, (hi - 1) // 1024 + 1)]

        with ExitStack() as sA:
            w_fm = sbt(sA, "w_fm", [128, KC, 1280], BF16)
            w_tm = sbt(sA, "w_tm", [128, KC, 840], BF16)
            win = dr["w_in"].ap().rearrange("(k p) n -> p k n", p=128)
            for k0 in range(0, KC, 4):
                ks = slice(k0, k0 + 4)
                DMA("pool", w_fm[:, ks, 0:640], win[:, ks, 0:640], (), ["w_fm"])
                DMA("pool", w_fm[:, ks, 640:704], win[:, ks, 576:640], (), ["w_fm"])
                DMA("pool", w_fm[:, ks, 704:768], win[:, ks, 512:576], (), ["w_fm"])
                DMA("pool", w_fm[:, ks, 768:1280], win[:, ks, 768:1280], (), ["w_fm"])
                DMA("pool", w_tm[:, ks, 0:512], win[:, ks, 1408:1920], (), ["w_tm"])
                DMA("pool", w_tm[:, ks, 512:640], win[:, ks, 640:768], (), ["w_tm"])
                DMA("pool", w_tm[:, ks, 640:768], win[:, ks, 1280:1408], (), ["w_tm"])
                DMA("pool", w_tm[:, ks, 768:840], win[:, ks, 1920:1992], (), ["w_tm"])
            w_o_sb = load_weight_bf16(sA, "w_o_sb", "w_o", D)
            wuv = sbt(sA, "wuv", [128, 8, 64], BF16)
            DMA("pool", wuv[:], dr["dsa_w_uv"].ap().rearrange("h c d -> c h d"), (), ["wuv"])

            bfm = sbt(sA, "bfm", [128, 10], F32)
            bin_ap = dr["b_in"].ap()
            fm_srcs = [(0, 512, 0), (512, 640, 4), (768, 1280, 6)]
            for lo, hi, j0 in fm_srcs:
                DMA("sp", bfm[:, j0:j0 + (hi - lo) // 128], bin_ap[lo:hi].rearrange("(j p) -> p j", p=128), (),
                    ["bfm"], allow_slow_non_contiguous=True)
            DMA("sp", bfm[0:64, 5:6], bin_ap[576:640].rearrange("(j p) -> p j", p=64), (), ["bfm"],
                allow_slow_non_contiguous=True)
            DMA("sp", bfm[64:128, 5:6], bin_ap[512:576].rearrange("(j p) -> p j", p=64), (), ["bfm"],
                allow_slow_non_contiguous=True)

            g0_bc = load_bcast(sA, "g0_bc", "ln_emb_g", D)
            b0_bc = load_bcast(sA, "b0_bc", "ln_emb_b", D)
            gkv_bc = load_bcast(sA, "gkv_bc", "dsa_kv_norm_g", 128)
            gik_bc = load_bcast(sA, "gik_bc", "idx_k_ln_g", 64)
            bik_bc = load_bcast(sA, "bik_bc", "idx_k_ln_b", 64)
            esink = load_bcast(sA, "esink", "swa_sinks", 8)
            ACT(esink[:], esink[:], AF.Exp, ["esink"], ["esink"])
            cmaskq = sbt(sA, "cmaskq", [128, 128], F32)
            DMA("sp", cmaskq[:], dr["c_cmaskq"].ap(), (), ["cmaskq"])
            pow2 = sbt(sA, "pow2", [128, NIT], F32)
            DMA("sp", pow2[:], dr["c_pow2"].ap(), (), ["pow2"])

            swaBT = sbt(sA, "swaBT", [128, 8, 2, 128], F32)
            dsaBT = sbt(sA, "dsaBT", [128, 8, 2, 128], F32)
            wukT = sbt(sA, "wukT", [128, 4, 128], BF16)
            ckv = sbt(sA, "ckv", [128, NT, 128], BF16)
            ckvT = sbt(sA, "ckvT", [128, T], BF16)
            ikT2 = sbt(sA, "ikT2", [128, T], BF16)
            kaTr = sbt(sA, "kaTr", [128, 2, 2, 128], BF16)
            va1r = sbt(sA, "va1r", [128, 2, 2, 65], BF16)
            MEMSET("dve", va1r[:], 1.0, ["va1r"])

            hl_tm = sbt(sA, "hl_tm", [2, 840], BF16)
            hl_o = sbt(sA, "hl_o", [2, D], BF16)
            with ExitStack() as sS:
                b2 = bin_ap.rearrange("(o n) -> o n", o=1)
                make_bias_hl(hl_tm, sS, "hl_tm", [(0, 512, b2[:, 1408:1920]), (512, 640, b2[:, 640:768]),
                                                     (640, 768, b2[:, 1280:1408]), (768, 840, b2[:, 1920:1992])], 840)
                make_bias_hl(hl_o, sS, "hl_o", [(0, D, dr["b_o"].ap().rearrange("(o n) -> o n", o=1))], D)
                tab = sbt(sS, "tab", [32, 16], F32)
                ohv = sbt(sS, "ohv", [32, 2, 384], F32)
                exch = sbt(sS, "exch", [128, 128], F32)
                maskT = sbt(sS, "maskT", [128, 2, 2, 128], F32)
                vecs = sbt(sS, "vecs", [16, 2, 384], F32)
                Hk = sbt(sS, "Hk", [128, 16, 2, 128], F32)
                wuk_f = sbt(sS, "wuk_f", [128, 8, 64], F32)
                DMA("sp", tab[:], dr["rel_bias"].ap(), (), ["tab"])
                DMA("sp", ohv[:], dr["c_ohv"].ap(), (), ["ohv"])
                DMA("sp", exch[:], dr["c_exch"].ap(), (), ["exch"])
                DMA("sp", maskT[:], dr["c_maskT"].ap(), (), ["maskT"])
                DMA("sp", wuk_f[:], dr["dsa_w_uk"].ap().rearrange("h c d -> c h d"), (), ["wuk_f"])
                for g in range(2):
                    MM(PB[g][0:16, 0:384], tab[:], ohv[:, g, :], True, True, ["tab", "ohv"], banks(g, 0, 384))
                    CP("dve", vecs[:, g, :], PB[g][0:16, 0:384], [], banks(g, 0, 384) + ["vecs"])
                DMA("sp", vec_d.ap()[0:8, :], vecs[0:8, 0, :], ["vecs"], ["vec_d"])
                DMA("sp", vec_d.ap()[8:16, :], vecs[8:16, 1, :], ["vecs"], ["vec_d"])
                for h0 in range(0, 16, 4):
                    src = bass.AP(vec_d, h0 * 384, [[1, 128], [384, 4], [128, 2], [1, 128]])
                    DMA("sp", Hk[:, h0:h0 + 4, :, :], src, ["vec_d"], ["Hk"])
                for blk in range(8):
                    pb = blk % 4
                    MM(PB[pb][:, 0:512], exch[:], Hk[:, 2 * blk:2 * blk + 2, :, :].rearrange("p h c q -> p (h c q)"),
                       True, True, ["exch", "Hk"], banks(pb, 0, 512))
                    g = 0 if blk < 4 else 1
                    for i in range(2):
                        if blk < 4:
                            dsl = swaBT[:, i * 4 + blk, :, :]
                        else:
                            dsl = dsaBT[:, 2 * (blk - 4) + i, :, :]
                        TT("dve", dsl, PB[pb][:, i * 256:(i + 1) * 256].rearrange("p (c q) -> p c q", c=2),
                           maskT[:, g, :, :], ALU.add, ["maskT"], banks(pb, 0, 512) + ["swaBT", "dsaBT"])
                for j in range(4):
                    MM(PB[j][:, 0:128], wuk_f[:, 2 * j:2 * j + 2, :].rearrange("p h d -> p (h d)"), ident_f[:],
                       True, True, ["wuk_f", "ident_f"], banks(j, 0, 128))
                    CP("dve", wukT[:, j, :], PB[j][:, 0:128], [], banks(j, 0, 128) + ["wukT"])
                P.barrier()
                P.emit()

            xin = [sbt(sA, "xin0", [128, D], F32)] * 2
            XN = [sbt(sA, f"XN{i}", [128, D], F32) for i in range(3)]
            Y = [sbt(sA, "Y0", [128, D], F32)] * 2
            bfA = sbt(sA, "bfA", [128, D], BF16)
            xT = sbt(sA, "xT", [128, KC, 128], BF16)
            AO = [sbt(sA, f"AO{i}", [128, D], BF16) for i in range(3)]
            aoT = sbt(sA, "aoT", [128, KC, 128], BF16)
            qaT = sbt(sA, "qaT", [128, 4, 128], BF16)
            qbT = sbt(sA, "qbT", [128, 4, 128], BF16)
            qlatT = [sbt(sA, f"qlatT{i}", [128, 8, 128], BF16) for i in range(3)]
            iqs = sbt(sA, "iqs", [128, 512], BF16)
            iqT = sbt(sA, "iqT", [128, 4, 128], BF16)
            ik2 = sbt(sA, "ik2", [128, 128], BF16)
            ikf = sbt(sA, "ikf", [128, 64], F32)
            sgnD = sbt(sA, "sgnD", [128, 8, 128], BF16)
            Rb = [sbt(sA, f"Rb{i}", [128, 512], BF16) for i in range(4)]
            score = sbt(sA, "score", [128, T], F32)
            NM = [sbt(sA, f"NM{i}", [128, T], BF16) for i in range(2)]
            LfS = sbt(sA, "LfS", [128, 1024], F32)
            PTS = sbt(sA, "PTS", [128, 2, 1024], BF16)
            PTD = sbt(sA, "PTD", [128, 3, 512], BF16)
            rs = sbt(sA, "rs", [128, 512], F32)
            olat = sbt(sA, "olat", [128, 1024], BF16)
            stF = sbt(sA, "stF", [128, 24], F32)
            stB = sbt(sA, "stB", [128, 24], F32)
            sm = sbt(sA, "sm", [128, 64], F32)
            bis = sbt(sA, "bis", [128, 8 + NIT], F32)
            print("pass A sbuf bytes remaining", nc.sbuf_bytes_remaining)

            x_ap = dr["x"].ap()
            PBt = PB[2]
            tmA = ["B5"]
            tmB = ["B6"]

            def front(n, part):
                slot = n % 2
                s2 = n % 3
                do_topk = n >= 2
                L = (n + 1) * 128
                if part == 2:
                    yield from front_indexer(n, do_topk, L)
                    return
                xi = xin[0]
                xk = "xin0"
                xn = XN[s2]
                xnk = f"XN{s2}"
                if part == 0:
                    DMA("sp", xi[:], x_ap[n * 128:(n + 1) * 128, :], (), [xk])
                    yield
                    layernorm(xi[:], [xk], xn[:], xnk, g0_bc[:], b0_bc[:], ["g0_bc", "b0_bc"], stF, "stF",
                              aff_eng="dve")
                    yield
                    ACT(bfA[:], xn[:], AF.Identity, [xnk], ["bfA"])
                    yield
                    return
                to_feature_major(xn[:], xnk, bfA[:], "bfA", xT, "xT", 2, ("act", "dve"), cast_eng=None)
                yield
                def fm_dst(j):
                    if j < 4:
                        return qaT[:, j, :], "qaT"
                    if j == 4:
                        return kaTr[:, 0, slot, :], f"kaTr{slot}"
                    if j == 5:
                        return kaTr[:, 1, slot, :], f"kaTr{slot}"
                    return qbT[:, j - 6, :], "qbT"

                def fm_chunk(j, pb, off):
                    for kc in range(KC):
                        MM(PB[pb][:, off:off + 128], w_fm[:, kc, j * 128:(j + 1) * 128], xT[:, kc, :], kc == 0,
                           kc == KC - 1, ["w_fm", "xT"], banks(pb, off, off + 128))

                def fm_evac(j, pb, off):
                    dst, dk = fm_dst(j)
                    if j % 2 == 0:
                        ACT(dst, PB[pb][:, off:off + 128], AF.Identity, ["bfm"], banks(pb, off, off + 128) + [dk],
                            bias=bfm[:, j:j + 1])
                    else:
                        TS("dve", dst, PB[pb][:, off:off + 128], bfm[:, j:j + 1], None, ALU.add, None, ["bfm"],
                           banks(pb, off, off + 128) + [dk])
                for j in range(8):
                    fm_chunk(j, 3, j * 128)
                    if j % 4 == 3:
                        yield
                for j in range(8, 10):
                    fm_chunk(j, 2, (j - 8) * 128)
                for j in range(10):
                    if j < 8:
                        fm_evac(j, 3, j * 128)
                    else:
                        fm_evac(j, 2, (j - 8) * 128)
                yield
                for (lo, hi, pb, off) in ((0, 512, 2, 512), (512, 840, 3, 0)):
                    w = hi - lo
                    o = PB[pb][:, off:off + w]
                    for kc in range(KC):
                        MM(o, xT[:, kc, :], w_tm[:, kc, lo:hi], kc == 0, False, ["xT", "w_tm"],
                           banks(pb, off, off + w))
                    bias_mm(o, hl_tm, lo, hi, "hl_tm", banks(pb, off, off + w))
                yield
                IQ = PB[2][:, 512:1024]
                TMB = PB[3]
                qb_ = {0: (3, 512), 1: (2, 0)}
                for h in range(8):
                    e = h % 2
                    pb, base = qb_[e]
                    off = base + (h // 2) * 128
                    MM(PB[pb][:, off:off + 128], wukT[64 * e:64 * e + 64, h // 2, :],
                       qbT[64 * e:64 * e + 64, h // 2, :], True, True, ["wukT", "qbT"], banks(pb, off, off + 128))
                for e in range(2):
                    pb, base = qb_[e]
                    ACT(qlatT[s2][:].rearrange("p (j e) q -> p e j q", e=2)[:, e, :, :],
                        PB[pb][:, base:base + 512].rearrange("p (j q) -> p j q", j=4), AF.Identity, [],
                        banks(pb, base, base + 512) + [f"qlatT{s2}"], scale=0.125)
                yield
                CP("dve", va1r[:, slot, :, 0:64], TMB[:, 0:128].rearrange("p (k d) -> p k d", k=2), [],
                   tmB + [f"va1r{slot}"])
                BNS(sm[:, 56:62], TMB[:, 128:256], ["sm"], tmB + ["sm"])
                BNA(sm[:, 62:64], sm[:, 56:62], ["sm"], ["sm"])
                STT(sm[:, 0:1], sm[:, 62:63], sm[:, 62:63], sm[:, 63:64], ALU.mult, ALU.add, ["sm"], ["sm"])
                rstd_from_var(sm[:, 0:1], sm[:, 2:3], sm[:, 1:2], ["sm"], ["sm"])
                STT(ckv[:, n, :], TMB[:, 128:256], sm[:, 2:3], gkv_bc[:], ALU.mult, ALU.mult, ["sm", "gkv_bc"],
                    tmB + [f"ckv{n}"])
                MM(PB[3][:, 512:640], ckv[:, n, :], ident_b[:], True, True, [f"ckv{n}", "ident_b"], ["B7"])
                CP("dve", ckvT[:, n * 128:(n + 1) * 128], PB[3][:, 512:640], [], ["B7", f"ckvT{n}"])
                BNS(sm[:, 8:14], TMB[:, 256:320], ["sm"], tmB + ["sm"])
                BNA(sm[:, 14:16], sm[:, 8:14], ["sm"], ["sm"])
                rstd_from_var(sm[:, 15:16], sm[:, 17:18], sm[:, 16:17], ["sm"], ["sm"])
                TS("dve", ikf[:], TMB[:, 256:320], sm[:, 14:15], sm[:, 17:18], ALU.subtract, ALU.mult, ["sm"],
                   tmB + ["ikf"])
                TT("dve", ikf[:], ikf[:], gik_bc[:], ALU.mult, ["ikf", "gik_bc"], ["ikf"])
                TT("dve", ik2[:, 0:64], ikf[:], bik_bc[:], ALU.add, ["ikf", "bik_bc"], ["ik2"])
                CP("dve", ik2[:, 64:128], ik2[:, 0:64], ["ik2"], ["ik2"])
                MM(PB[3][:, 640:768], ik2[:], ident_b[:], True, True, ["ik2", "ident_b"], ["B7"])
                CP("dve", ikT2[:, n * 128:(n + 1) * 128], PB[3][:, 640:768], [], ["B7", f"ikT{n}"])
                yield
                do_topk = n >= 2
                if do_topk:
                    ACT(sm[:, 24:32], TMB[:, 320:328], AF.Abs, ["sm"], tmB + ["sm"])
                    ACT(sm[:, 32:40], TMB[:, 320:328], AF.Sign, ["sm"], tmB + ["sm"])
                    TT("dve", iqs[:].rearrange("p (h d) -> p h d", h=8), IQ.rearrange("p (h d) -> p h d", h=8),
                       sm[:, 24:32].rearrange("p (h o) -> p h o", o=1).to_broadcast([128, 8, 64]), ALU.mult,
                       ["sm"], tmA + ["iqs"])
                    TT("dve", sgnD[:], ident_b[:].rearrange("p (o q) -> p o q", o=1).to_broadcast([128, 8, 128]),
                       sm[:, 32:40].rearrange("p (h o) -> p h o", o=1).to_broadcast([128, 8, 128]), ALU.mult,
                       ["sm", "ident_b"], ["sgnD"])
                    for j in range(4):
                        MM(PB[2][:, j * 128:(j + 1) * 128], iqs[:, j * 128:(j + 1) * 128], ident_b[:], True, True,
                           ["iqs", "ident_b"], ["B4"])
                    ACT(iqT[:].rearrange("p j q -> p (j q)"), PB[2][:, 0:512], AF.Identity, [], ["B4", "iqT"])
                    yield
                chunks = [(0, n)] + ([(1, n - 1)] if n >= 1 else [])
                for (c, kt) in chunks:
                    ks = kt % 2
                    for h in range(8):
                        e = h % 2
                        kv = h // 4
                        arr = 0 if kv == e else 1
                        hh = e * 4 + h // 2
                        MM(PB[3][:, hh * 128:(hh + 1) * 128], kaTr[64 * e:64 * e + 64, arr, ks, :],
                           qaT[64 * e:64 * e + 64, h // 2, :], True, True, [f"kaTr{ks}", "qaT"],
                           banks(3, hh * 128, (hh + 1) * 128))
                    STT(LfS[:].rearrange("p (h q) -> p h q", h=8), PB[3][:].rearrange("p (h q) -> p h q", h=8),
                        0.125, swaBT[:, :, c, :], ALU.mult, ALU.add, ["swaBT"], ["B6", "B7", "LfS"])
                    ACT(PTS[:, c, :], LfS[:], AF.Exp, ["LfS"], [f"PTS{c}"])
                    yield
                for h in range(8):
                    kv = h // 4
                    hh = (h % 2) * 4 + h // 2
                    for ci, (c, kt) in enumerate(chunks):
                        MM(PB[2][:, h * 128:h * 128 + 65], PTS[:, c, hh * 128:(hh + 1) * 128],
                           va1r[:, kt % 2, kv, :], ci == 0, ci == len(chunks) - 1, [f"PTS{c}", f"va1r{kt % 2}"],
                           banks(2, h * 128, h * 128 + 65))
                pv = PB[2][:].rearrange("p (h x) -> p h x", h=8)
                TT("dve", sm[:, 40:48].rearrange("p (h o) -> p h o", o=1), pv[:, :, 64:65],
                   esink[:].rearrange("p (h o) -> p h o", o=1), ALU.add, ["esink", "sm"], ["B4", "B5", "sm"])
                RECIP(sm[:, 48:56], sm[:, 40:48], ["sm"], ["sm"])
                TT("dve", AO[s2][:, 0:512].rearrange("p (h d) -> p h d", h=8), pv[:, :, 0:64],
                   sm[:, 48:56].rearrange("p (h o) -> p h o", o=1).to_broadcast([128, 8, 64]), ALU.mult, ["sm"],
                   ["B4", "B5", f"AO{s2}"])
                yield

            def front_indexer(n, do_topk, L):
                if do_topk:
                    nblk = (L + 511) // 512
                    items = [(kb, h) for kb in range(nblk) for h in range(8)]

                    def geom(kb):
                        w = min(512, L - kb * 512)
                        soff = 512 * (kb % 2)
                        return w, soff, banks(2, soff, soff + 512)

                    dbank = [(3, 0), (3, 512), (0, 0), (0, 512)]

                    def dots(i):
                        kb, h = items[i]
                        w, soff, sk = geom(kb)
                        e = h % 2
                        dpb, doff = dbank[i % 4]
                        dk = banks(dpb, doff, doff + 512)
                        MM(PB[dpb][:, doff:doff + w], iqT[64 * e:64 * e + 64, h // 2, :],
                           ikT2[64 * e:64 * e + 64, kb * 512:kb * 512 + w], True, True,
                           ["iqT"] + [f"ikT{t}" for t in range(kb * 4, min(n + 1, kb * 4 + 4))], dk)

                    def relu(i):
                        kb, h = items[i]
                        w, soff, sk = geom(kb)
                        dpb, doff = dbank[i % 4]
                        dk = banks(dpb, doff, doff + 512)
                        ACT(Rb[i % 4][:, 0:w], PB[dpb][:, doff:doff + w], AF.Relu, [], dk + [f"Rb{i % 4}"])

                    def accum(i):
                        kb, h = items[i]
                        w, soff, sk = geom(kb)
                        MM(PB[2][:, soff:soff + w], sgnD[:, h, :], Rb[i % 4][:, 0:w], h == 0, h == 7,
                           ["sgnD", f"Rb{i % 4}"], sk)
                        if h == 7:
                            if kb == nblk - 1:
                                wd = w - 128
                                if wd > 0:
                                    CP("dve", score[:, kb * 512:kb * 512 + wd], PB[2][:, soff:soff + wd], [],
                                       sk + ["score"])
                                TT("dve", score[:, L - 128:L], PB[2][:, soff + wd:soff + w], cmaskq[:], ALU.add,
                                   ["cmaskq"], sk + ["score"])
                            else:
                                CP("dve", score[:, kb * 512:kb * 512 + 512], PB[2][:, soff:soff + 512], [],
                                   sk + ["score"])
                    npair = len(items) // 2
                    for pi in range(npair + 1):
                        if pi < npair:
                            dots(2 * pi)
                            dots(2 * pi + 1)
                            relu(2 * pi)
                            relu(2 * pi + 1)
                        if pi >= 1:
                            accum(2 * pi - 2)
                            accum(2 * pi - 1)
                        yield
                    yield

            def middle(n):
                if n < 2:
                    return
                s2 = n % 2
                L = (n + 1) * 128
                nm = NM[s2]
                nmk = f"NM{s2}"
                bk = ["bis"]
                RED(bis[:, 0:1], score[:, 0:L], ALU.max, ["score"], bk)
                RED(bis[:, 1:2], score[:, 0:L - 128], ALU.min, ["score"], bk)
                yield
                TT("dve", bis[:, 2:3], bis[:, 0:1], bis[:, 1:2], ALU.subtract, bk, bk)
                TS("dve", bis[:, 8:8 + NIT], pow2[:], bis[:, 2:3], None, ALU.mult, None, bk + ["pow2"], bk)
                TT("dve", bis[:, 3:4], bis[:, 1:2], bis[:, 8:9], ALU.add, bk, bk)
                for it in range(NIT):
                    TS("dve", nm[:, 0:L], score[:, 0:L], bis[:, 3:4], 0.0, ALU.is_ge, ALU.add, ["score"] + bk,
                       [nmk] + bk, accum_out=bis[:, 4:5])
                    TS("dve", bis[:, 5:6], bis[:, 4:5], TOPK - 0.5, bis[:, 8 + it:9 + it], ALU.is_ge, ALU.mult, bk, bk)
                    nxt = min(it + 1, NIT - 1)
                    STT(bis[:, 3:4], bis[:, 3:4], bis[:, 8 + nxt:9 + nxt], bis[:, 5:6], ALU.subtract, ALU.add, bk, bk)
                    yield
                TS("dve", nm[:, 0:L], score[:, 0:L], bis[:, 3:4], NEGM, ALU.is_lt, ALU.mult, ["score"] + bk, [nmk])
                yield

            def back(n):
                s2 = n % 3
                do_topk = n >= 2
                nm = NM[n % 2]
                nmk = f"NM{n % 2}"
                ql = qlatT[s2]
                qlk = f"qlatT{s2}"
                ao = AO[s2]
                aok = f"AO{s2}"
                items = [(hf, j) for hf in range(2) for j in range(n + 1)]

                def logits(i):
                    hf, j = items[i]
                    par = i % 2
                    p3 = i % 3
                    near = j >= n - 1
                    lo = PB[0][:, 512 * par:512 * par + 512]
                    lk = [f"B{par}"]
                    MM(lo, ckvT[:, j * 128:(j + 1) * 128], ql[:, 4 * hf:4 * hf + 4, :].rearrange("p h q -> p (h q)"),
                       True, not do_topk, [f"ckvT{j}", qlk], lk)
                    if do_topk:
                        MM(lo, nm[:, j * 128:(j + 1) * 128], ident4[:].rearrange("p r q -> p (r q)"), False, True,
                           [nmk, "ident4"], lk)
                    if near:
                        c = 0 if j == n else 1
                        lfd = LfS[:, 512 * par:512 * par + 512]
                        TT("dve", lfd.rearrange("p (h q) -> p h q", h=4), lo.rearrange("p (h q) -> p h q", h=4),
                           dsaBT[:, 4 * hf:4 * hf + 4, c, :], ALU.add, ["dsaBT"], lk + ["LfS"])
                        ACT(PTD[:, p3, :], lfd, AF.Exp, ["LfS"], [f"PTD{p3}"])
                    else:
                        ACT(PTD[:, p3, :], lo, AF.Exp, [], lk + [f"PTD{p3}"])

                def pv(i):
                    hf, j = items[i]
                    p3 = i % 3
                    MM(PB[1][:, 0:512], ckv[:, j, :], PTD[:, p3, :], j == 0, j == n, [f"ckv{j}", f"PTD{p3}"], ["B2"])
                    MM(PB[1][:, 512:1024], ones_b[:], PTD[:, p3, :], j == 0, j == n, ["ones_b", f"PTD{p3}"], ["B3"])
                    if j == n:
                        RECIP_ACT(rs[:], PB[1][:, 512:1024], rs[:], [], ["B3", "rs"])
                        TT("dve", olat[:, 512 * hf:512 * hf + 512], PB[1][:, 0:512], rs[:], ALU.mult, ["rs"],
                           ["B2", "olat"])
                for i in range(len(items) + 1):
                    if i < len(items):
                        logits(i)
                    if i >= 1:
                        pv(i - 1)
                    yield
                for h in range(8):
                    MM(PB[0][:, h * 64:(h + 1) * 64], olat[:, h * 128:(h + 1) * 128], wuv[:, h, :], True, True,
                       ["olat", "wuv"], ["B0"])
                ACT(ao[:, 512:1024], PB[0][:, 0:512], AF.Identity, [], ["B0", aok])
                yield
                for kc in range(KC):
                    MM(PB[0][:, kc * 128:(kc + 1) * 128], ao[:, kc * 128:(kc + 1) * 128], ident_b[:], True, True,
                       [aok, "ident_b"], banks(0, kc * 128, (kc + 1) * 128))
                ACT(aoT[:, 0:4, :].rearrange("p a b -> p (a b)"), PB[0][:, 0:512], AF.Identity, [], ["B0", "aoT"])
                CP("dve", aoT[:, 4:8, :].rearrange("p a b -> p (a b)"), PB[0][:, 512:1024], [], ["B1", "aoT"])
                yield
                for hf in range(2):
                    o = PB[1][:, 512 * hf:512 * hf + 512]
                    for kc in range(KC):
                        MM(o, aoT[:, kc, :], w_o_sb[:, kc, 512 * hf:512 * hf + 512], kc == 0, False,
                           ["aoT"] + wkeys("w_o_sb", kc, 512 * hf, 512 * hf + 512), banks(1, 512 * hf, 512 * hf + 512))
                    bias_mm(o, hl_o, 512 * hf, 512 * hf + 512, "hl_o", banks(1, 512 * hf, 512 * hf + 512))
                    yield
                y = Y[0]
                yk = "Y0"
                STT(y[:], XN[s2][:], ALPHA, PB[1][:], ALU.mult, ALU.add, [f"XN{s2}"], ["B2", "B3", yk])
                layernorm(y[:], [yk], y[:], yk, None, None, [], stB, "stB", affine=False)
                DMA("sp", x1_d.ap()[n * 128:(n + 1) * 128, :], y[:], [yk], [f"x1d{n}"])
                yield

            def run_interleaved(gens):
                gens = [g for g in gens if g is not None]
                while gens:
                    alive = []
                    for g in gens:
                        try:
                            next(g)
                            alive.append(g)
                        except StopIteration:
                            pass
                    gens = alive

            nA = n_tiles if dbg >= 2 else 0
            if nA > 0:
                run_interleaved([front(0, 0)])
                run_interleaved([front(0, 1)])
                run_interleaved([front(0, 2)])
            def chain(*gens):
                for g in gens:
                    if g is not None:
                        yield from g

            def run_weighted(gm, gmain, ratio):
                alive_m, alive_x = gm is not None, gmain is not None
                while alive_m or alive_x:
                    if alive_m:
                        try:
                            next(gm)
                        except StopIteration:
                            alive_m = False
                    for _ in range(ratio if alive_m else 10 ** 6):
                        if not alive_x:
                            break
                        try:
                            next(gmain)
                        except StopIteration:
                            alive_x = False

            for t in range(nA + 1):
                gm = middle(t) if (t < nA and t >= 2) else None
                gb = back(t - 1) if t >= 1 else None
                gf = front(t + 1, 1) if t + 1 < nA else None
                gh = front(t + 1, 0) if t + 1 < nA else None
                len_m = NIT + 2
                len_x = (2 * t + 10 if t >= 1 else 0) + (22 if gf is not None else 0)
                ratio = max(1, -(-len_x // len_m))
                run_weighted(gm, chain(gh, gb, gf), ratio)
                if t + 1 < nA:
                    run_interleaved([front(t + 1, 2)])
            P.barrier()
            P.emit()

        if dbg > 8:
            with ExitStack() as sB:
                wq_sb = load_weight_bf16(sB, "wq_sb", "xa_wq", D)
                xwo_sb = load_weight_bf16(sB, "xwo_sb", "xa_wo", D)
                g1_bc = load_bcast(sB, "g1_bc", "ln1_g", D)
                b1_bc = load_bcast(sB, "b1_bc", "ln1_b", D)
                g2_bc = load_bcast(sB, "g2_bc", "ln2_g", D)
                b2_bc = load_bcast(sB, "b2_bc", "ln2_b", D)
                brow_v = sbt(sB, "brow_v", [1, D], F32)
                DMA("sp", brow_v[:], dr["xa_bkv"].ap()[D:2 * D].rearrange("(o n) -> o n", o=1), (), ["brow_v"])
                bq16 = sbt(sB, "bq16", [128, 8], F32)
                DMA("sp", bq16[:], dr["xa_bq"].ap().rearrange("(j p) -> p j", p=128), (), ["bq16"],
                    allow_slow_non_contiguous=True)
                TS("dve", bq16[:], bq16[:], 1.0 / 16.0, None, ALU.mult, None, ["bq16"], ["bq16"])
                bkc = sbt(sB, "bkc", [128, 8], F32)
                DMA("sp", bkc[:], dr["xa_bkv"].ap()[0:D].rearrange("(j p) -> p j", p=128), (), ["bkc"],
                    allow_slow_non_contiguous=True)
                hl_xo = sbt(sB, "hl_xo", [2, D], BF16)
                kmT = sbt(sB, "kmT", [128, 8, 256], BF16)
                vm = sbt(sB, "vm", [128, 2, D], BF16)
                with ExitStack() as sS:
                    make_bias_hl(hl_xo, sS, "hl_xo", [(0, D, dr["xa_bo"].ap().rearrange("(o n) -> o n", o=1))], D)
                    wk_sb = load_weight_bf16(sS, "wk_sb", "xa_wkv", 2 * D, 0, D)
                    wv_sb = load_weight_bf16(sS, "wv_sb", "xa_wkv", 2 * D, D, 2 * D)
                    memb = sbt(sS, "memb", [128, 2, D], BF16)
                    memT = sbt(sS, "memT", [128, KC, 256], BF16)
                    DMA("pool", memb[:], dr["mem"].ap().rearrange("(c p) d -> p c d", p=128), (), ["memb"])
                    for mc in range(2):
                        for kc in range(KC):
                            MM(PB[mc][:, kc * 128:(kc + 1) * 128], memb[:, mc, kc * 128:(kc + 1) * 128], ident_b[:],
                               True, True, ["memb", "ident_b"], banks(mc, kc * 128, (kc + 1) * 128))
                        CP("dve", memT[:, :, mc * 128:(mc + 1) * 128], PB[mc][:].rearrange("p (k m) -> p k m", k=8), [],
                           banks(mc) + ["memT"])
                    for fc in range(8):
                        pb, off = 2 + fc // 4, (fc % 4) * 256
                        for kc in range(KC):
                            MM(PB[pb][:, off:off + 256], wk_sb[:, kc, fc * 128:(fc + 1) * 128], memT[:, kc, :], kc == 0,
                               kc == KC - 1, wkeys("wk_sb", kc, fc * 128, (fc + 1) * 128) + ["memT"],
                               banks(pb, off, off + 256))
                        ACT(kmT[:, fc, :], PB[pb][:, off:off + 256], AF.Identity, ["bkc"],
                            banks(pb, off, off + 256) + ["kmT"], bias=bkc[:, fc:fc + 1])
                    for mc in range(2):
                        for hf in range(2):
                            o = PB[mc][:, 512 * hf:512 * hf + 512]
                            for kc in range(KC):
                                MM(o, memT[:, kc, mc * 128:(mc + 1) * 128], wv_sb[:, kc, 512 * hf:512 * hf + 512],
                                   kc == 0, False, ["memT"] + wkeys("wv_sb", kc, 512 * hf, 512 * hf + 512),
                                   banks(mc, 512 * hf, 512 * hf + 512))
                            MM(o, ones_f[0:1, :], brow_v[:, 512 * hf:512 * hf + 512], False, True, ["ones_f", "brow_v"],
                               banks(mc, 512 * hf, 512 * hf + 512))
                        CP("dve", vm[:, mc, :], PB[mc][:], [], banks(mc) + ["vm"])
                    P.barrier()
                    P.emit()

                WB = 4
                x1in = [sbt(sB, f"x1in{i}", [128, D], F32) for i in range(2 * WB)]
                Yb = [sbt(sB, f"Yb{i}", [128, D], F32) for i in range(WB)]
                bfB = [sbt(sB, f"bfB{i}", [128, D], BF16) for i in range(WB)]
                x1T = [sbt(sB, f"x1T{i}", [128, KC, 128], BF16) for i in range(WB)]
                xqT = [sbt(sB, f"xqT{i}", [128, KC, 128], BF16) for i in range(WB)]
                PTx = [sbt(sB, f"PTx{i}", [128, 1024], BF16) for i in range(WB)]
                rbc = [sbt(sB, f"rbc{i}", [128, 512], F32) for i in range(WB)]
                caoT = [sbt(sB, f"caoT{i}", [128, KC, 128], BF16) for i in range(WB)]
                stb = [sbt(sB, f"stb{i}", [128, 24], F32) for i in range(WB)]
                print("pass B sbuf bytes remaining", nc.sbuf_bytes_remaining)

                def load_x1(m):
                    DMA("sp", x1in[m % (2 * WB)][:], x1_d.ap()[m * 128:(m + 1) * 128, :], [f"x1d{m}"],
                        [f"x1in{m % (2 * WB)}"])

                for m in range(min(WB, n_tiles)):
                    load_x1(m)

                def tileB(n):
                    p = n % WB
                    pa = p
                    xi = x1in[n % (2 * WB)]
                    xk = f"x1in{n % (2 * WB)}"
                    if n + WB < n_tiles:
                        load_x1(n + WB)
                    TT("dve", xi[:], xi[:], g1_bc[:], ALU.mult, [xk, "g1_bc"], [xk])
                    TT("dve", xi[:], xi[:], b1_bc[:], ALU.add, [xk, "b1_bc"], [xk])
                    yield
                    to_feature_major(xi[:], xk, bfB[p][:], f"bfB{p}", x1T[p], f"x1T{p}", pa, ("act", "dve"),
                                     cast_eng="dve")
                    yield
                    for fc in range(8):
                        o = PB[pa][:, fc * 128:(fc + 1) * 128]
                        for kc in range(KC):
                            MM(o, wq_sb[:, kc, fc * 128:(fc + 1) * 128], x1T[p][:, kc, :], kc == 0, kc == KC - 1,
                               wkeys("wq_sb", kc, fc * 128, (fc + 1) * 128) + [f"x1T{p}"],
                               banks(pa, fc * 128, (fc + 1) * 128))
                        if fc % 2 == 1:
                            yield
                    for fc in range(8):
                        ACT(xqT[p][:, fc, :], PB[pa][:, fc * 128:(fc + 1) * 128], AF.Identity, ["bq16"],
                            banks(pa, fc * 128, (fc + 1) * 128) + [f"xqT{p}"], bias=bq16[:, fc:fc + 1],
                            scale=1.0 / 16.0)
                    yield
                    for mc in range(2):
                        for h in range(4):
                            o = PB[pa][:, (mc * 4 + h) * 128:(mc * 4 + h + 1) * 128]
                            for kk in range(2):
                                MM(o, kmT[:, 2 * h + kk, mc * 128:(mc + 1) * 128], xqT[p][:, 2 * h + kk, :], kk == 0,
                                   kk == 1, ["kmT", f"xqT{p}"], banks(pa, (mc * 4 + h) * 128, (mc * 4 + h + 1) * 128))
                        yield
                    ACT(PTx[p][:], PB[pa][:], AF.Exp, [], banks(pa) + [f"PTx{p}"])
                    yield
                    for mc in range(2):
                        MM(PB[pa][:, 0:512], ones_b[:], PTx[p][:, mc * 512:(mc + 1) * 512], mc == 0, mc == 1,
                           ["ones_b", f"PTx{p}"], banks(pa, 0, 512))
                    RECIP_ACT(rbc[p][:], PB[pa][:, 0:512], rbc[p][:], [], banks(pa, 0, 512) + [f"rbc{p}"])
                    yield
                    for fc in range(8):
                        h = fc // 2
                        o = PB[pa][:, fc * 128:(fc + 1) * 128]
                        for mc in range(2):
                            MM(o, vm[:, mc, fc * 128:(fc + 1) * 128],
                               PTx[p][:, (mc * 4 + h) * 128:(mc * 4 + h + 1) * 128], mc == 0, mc == 1,
                               ["vm", f"PTx{p}"], banks(pa, fc * 128, (fc + 1) * 128))
                        if fc % 4 == 3:
                            yield
                    TT("dve", caoT[p][:].rearrange("p (h t) q -> p h t q", h=4),
                       PB[pa][:].rearrange("p (h t q) -> p h t q", h=4, t=2),
                       rbc[p][:].rearrange("p (h o q) -> p h o q", h=4, o=1).to_broadcast([128, 4, 2, 128]), ALU.mult,
                       [f"rbc{p}"], banks(pa) + [f"caoT{p}"])
                    yield
                    for hf in range(2):
                        o = PB[pa][:, 512 * hf:512 * hf + 512]
                        for kc in range(KC):
                            MM(o, caoT[p][:, kc, :], xwo_sb[:, kc, 512 * hf:512 * hf + 512], kc == 0, False,
                               [f"caoT{p}"] + wkeys("xwo_sb", kc, 512 * hf, 512 * hf + 512),
                               banks(pa, 512 * hf, 512 * hf + 512))
                        bias_mm(o, hl_xo, 512 * hf, 512 * hf + 512, "hl_xo", banks(pa, 512 * hf, 512 * hf + 512))
                        yield
                    STT(Yb[p][:], xi[:], ALPHA, PB[pa][:], ALU.mult, ALU.add, [xk], banks(pa) + [f"Yb{p}"])
                    yield
                    layernorm(Yb[p][:], [f"Yb{p}"], Yb[p][:], f"Yb{p}", g2_bc[:], b2_bc[:], ["g2_bc", "b2_bc"], stb[p],
                              f"stb{p}", aff_eng="dve")
                    DMA("sp", x2_d.ap()[n * 128:(n + 1) * 128, :], Yb[p][:], [f"Yb{p}"], [f"x2d{n}"])
                    yield

                run_pipeline(tileB, n_tiles, WB, 6)
                P.barrier()
                P.emit()

        if dbg > 9:
            with ExitStack() as sC:
                wup = sbt(sC, "wup", [128, KC, DFF], BF16)
                wdn = sbt(sC, "wdn", [128, 32, D], BF16)
                g3_bc = load_bcast(sC, "g3_bc", "ln3_g", D)
                b3_bc = load_bcast(sC, "b3_bc", "ln3_b", D)
                hl_d = sbt(sC, "hl_d", [2, D], BF16)
                bupc = sbt(sC, "bupc", [128, 32], F32)
                DMA("sp", bupc[:], dr["b_up"].ap().rearrange("(j p) -> p j", p=128), (), ["bupc"],
                    allow_slow_non_contiguous=True)
                with ExitStack() as sS:
                    make_bias_hl(hl_d, sS, "hl_d", [(0, D, dr["b_down"].ap().rearrange("(o n) -> o n", o=1))], D)
                    P.barrier()
                    P.emit()
                load_weight_into(wup, "wup", "w_up", DFF, D)
                load_weight_into(wdn, "wdn", "w_down", D, DFF)
                G = 2
                x2in = [sbt(sC, f"x2in{i}", [128, D], F32) for i in range(2 * G)]
                bfC = sbt(sC, "bfC", [128, D], BF16)
                x2T = [sbt(sC, f"x2T{i}", [128, KC, G * 128], BF16) for i in range(2)]
                x2Tt = sbt(sC, "x2Tt", [128, KC, 128], BF16)
                rT = sbt(sC, "rT", [128, 2, G * 128], BF16)
                gT = sbt(sC, "gT", [128, 32, G * 128], BF16)
                Yc = [sbt(sC, f"Yc{i}", [128, D], F32) for i in range(2)]
                stc = sbt(sC, "stc", [128, 24], F32)
                print("pass C sbuf bytes remaining", nc.sbuf_bytes_remaining)
                ngrp = (n_tiles + G - 1) // G

                def prepC(gi):
                    gp = gi % 2
                    tiles = list(range(gi * G, min(n_tiles, (gi + 1) * G)))
                    for ti, n in enumerate(tiles):
                        bi = gp * G + ti
                        xi = x2in[bi]
                        xk = f"x2in{bi}"
                        DMA("sp", xi[:], x2_d.ap()[n * 128:(n + 1) * 128, :], [f"x2d{n}"], [xk])
                        yield
                        to_feature_major(xi[:], xk, bfC[:], "bfC", x2T[gp][:, :, ti * 128:(ti + 1) * 128],
                                         f"x2T{gp}", 0, ("act", "dve"), cast_eng="dve")
                        yield

                def mlpC(gi):
                    gp = gi % 2
                    tiles = list(range(gi * G, min(n_tiles, (gi + 1) * G)))
                    ntk = len(tiles) * 128
                    for fc in range(32):
                        off = 512 * (fc % 2)
                        o = PB[1][:, off:off + ntk]
                        for kc in range(KC):
                            MM(o, wup[:, kc, fc * 128:(fc + 1) * 128], x2T[gp][:, kc, 0:ntk], kc == 0, kc == KC - 1,
                               wkeys("wup", kc, fc * 128, (fc + 1) * 128) + [f"x2T{gp}"], banks(1, off, off + ntk))
                        ACT(rT[:, fc % 2, 0:ntk], o, AF.Relu, ["bupc"], banks(1, off, off + ntk) + [f"rT{fc % 2}"],
                            bias=bupc[:, fc:fc + 1])
                        TT("dve" if fc % 4 != 3 else "pool", gT[:, fc, 0:ntk], rT[:, fc % 2, 0:ntk], rT[:, fc % 2, 0:ntk],
                           ALU.mult, [f"rT{fc % 2}"], [f"gT{fc}"])
                        if fc % 2 == 1:
                            yield
                    for ti, n in enumerate(tiles):
                        bi = gp * G + ti
                        xi = x2in[bi]
                        xk = f"x2in{bi}"
                        yc = Yc[n % 2]
                        yk = f"Yc{n % 2}"
                        dp = 3 if ti == 0 else 2
                        for hf in range(2):
                            o = PB[dp][:, 512 * hf:512 * hf + 512]
                            for fc in range(32):
                                MM(o, gT[:, fc, ti * 128:(ti + 1) * 128], wdn[:, fc, 512 * hf:512 * hf + 512], fc == 0,
                                   False, [f"gT{fc}"] + wkeys("wdn", fc, 512 * hf, 512 * hf + 512),
                                   banks(dp, 512 * hf, 512 * hf + 512))
                                if fc % 8 == 7:
                                    yield
                            bias_mm(o, hl_d, 512 * hf, 512 * hf + 512, "hl_d", banks(dp, 512 * hf, 512 * hf + 512))
                        STT(yc[:], xi[:], ALPHA, PB[dp][:], ALU.mult, ALU.add, [xk], banks(dp) + [yk])
                        layernorm(yc[:], [yk], yc[:], yk, g3_bc[:], b3_bc[:], ["g3_bc", "b3_bc"], stc, "stc",
                                  aff_eng="dve")
                        DMA("sp", y_d.ap()[n * 128:(n + 1) * 128, :], yc[:], [yk], [f"yd{n}"])
                        yield

                def run_il(gens):
                    gens = list(gens)
                    while gens:
                        alive = []
                        for g in gens:
                            try:
                                next(g)
                                alive.append(g)
                            except StopIteration:
                                pass
                        gens = alive

                if ngrp > 0:
                    run_il([prepC(0)])
                for gi in range(ngrp):
                    gs = [mlpC(gi)]
                    if gi + 1 < ngrp:
                        gs.append(prepC(gi + 1))
                    run_il(gs)
                P.op("sp", None, [f"yd{n}" for n in range(n_tiles)], ())
                P.emit()
    return nc


_NC_CACHE = {}


def kernel(**inputs):
    consts = _constants()
    if "nc" not in _NC_CACHE:
        _NC_CACHE["nc"] = build_nc()
    nc = _NC_CACHE["nc"]
    in_maps = []
    for c in range(N_CORES):
        m = {}
        for k in PARAM_SHAPES:
            a = np.asarray(inputs[k], dtype=np.float32)
            if k == "x" or k == "mem":
                a = a[c]
            elif k != "rel_bias" and k not in ("ln_emb_g", "ln_emb_b"):
                a = a[0]
            m[k] = np.ascontiguousarray(a).reshape(PARAM_SHAPES[k])
        m.update(consts)
        in_maps.append(m)
    res = run_bass_kernel_spmd(nc, in_maps, core_ids=list(range(N_CORES)))
    return np.stack([np.asarray(r["y"], dtype=np.float32) for r in res.results], axis=0)
```
